# Optimizing a Trainium2 kernel written in Bass

```python
import math
import jax, jax.numpy as jnp
from jax import lax
import numpy as np


D_MODEL = 1024
BATCH = 16
SEQ = 2048
DEPTH = 2
DEC_BATCH = 128
DEC_SEQ = 1
PAST_LEN = 16384
PAGE_SIZE = 128

N_HEADS = 16
N_KV_HEADS = 2
HEAD_DIM = 64
GROUP = N_HEADS // N_KV_HEADS
WINDOW = 128
BLOCK = WINDOW
W_BUF = min(WINDOW, PAST_LEN)
ROPE_THETA = 10000.0
ATTN_SCALE = HEAD_DIM ** -0.5
GLA_HEADS = 4
GLA_DK = D_MODEL // 2
GLA_DV = D_MODEL
GLA_DKH = GLA_DK // GLA_HEADS
GLA_DVH = GLA_DV // GLA_HEADS
GLA_RANK = 16
GLA_TAU = 16.0
GLA_CHUNK = 64
D_FF = 2816
CONV_W = 3
N_ATTN = (DEPTH + 1) // 2
N_GLA = DEPTH // 2
ALPHA = (2 * DEPTH) ** 0.25
BETA = (8 * DEPTH) ** -0.25
LN_EPS = 1e-5
NORM_EPS = 1e-6
NEG_INF = -1e30

kernel_name = 'swa_sink_gla_convffn_deepnorm_step'


def layer_norm(x, g, b):
    xf = x.astype(jnp.float32)
    mu = jnp.mean(xf, -1, keepdims=True)
    var = jnp.mean(jnp.square(xf - mu), -1, keepdims=True)
    return ((xf - mu) * lax.rsqrt(var + LN_EPS) * g.astype(jnp.float32) + b.astype(jnp.float32)).astype(x.dtype)


def rope(x, pos):
    inv = 1.0 / (ROPE_THETA ** (jnp.arange(0, HEAD_DIM, 2, dtype=jnp.float32) / HEAD_DIM))
    ang = pos[:, None] * inv[None, :]
    cos = jnp.concatenate([jnp.cos(ang), jnp.cos(ang)], -1)[:, None, :]
    sin = jnp.concatenate([jnp.sin(ang), jnp.sin(ang)], -1)[:, None, :]
    xf = x.astype(jnp.float32)
    x1, x2 = jnp.split(xf, 2, axis=-1)
    rot = jnp.concatenate([-x2, x1], -1)
    return (xf * cos + rot * sin).astype(x.dtype)


def sink_softmax(scores, mask, sink):
    s = jnp.where(mask, scores, NEG_INF)
    m = jnp.maximum(jnp.max(s, -1, keepdims=True), sink)
    p = jnp.exp(s - m)
    return p / (jnp.sum(p, -1, keepdims=True) + jnp.exp(sink - m))


def qkv_proj(x, w_qkv, b_qkv, offset):
    B, T, _ = x.shape
    nq = N_HEADS * HEAD_DIM
    nk = N_KV_HEADS * HEAD_DIM
    qkv = x @ w_qkv + b_qkv
    q = qkv[..., :nq].reshape(B, T, N_HEADS, HEAD_DIM)
    k = qkv[..., nq:nq + nk].reshape(B, T, N_KV_HEADS, HEAD_DIM)
    v = qkv[..., nq + nk:].reshape(B, T, N_KV_HEADS, HEAD_DIM)
    pos = jnp.arange(T, dtype=jnp.float32) + offset
    return rope(q, pos), rope(k, pos), v


def swa_prompt(x, w_qkv, b_qkv, sinks, w_o):
    B, T, _ = x.shape
    q, k, v = qkv_proj(x, w_qkv, b_qkv, 0)
    nb = T // BLOCK
    qb = q.reshape(B, nb, BLOCK, N_KV_HEADS, GROUP, HEAD_DIM)
    kb = k.reshape(B, nb, BLOCK, N_KV_HEADS, HEAD_DIM)
    vb = v.reshape(B, nb, BLOCK, N_KV_HEADS, HEAD_DIM)

    def with_prev(t):
        prev = jnp.pad(t, ((0, 0), (1, 0), (0, 0), (0, 0), (0, 0)))[:, :nb]
        return jnp.concatenate([prev, t], axis=2)

    kk, vv = with_prev(kb), with_prev(vb)
    scores = jnp.einsum('bnqkgd,bnskd->bnkgqs', qb, kk).astype(jnp.float32) * ATTN_SCALE
    blk = jnp.arange(nb)[:, None, None]
    qpos = blk * BLOCK + jnp.arange(BLOCK)[None, :, None]
    kpos = (blk - 1) * BLOCK + jnp.arange(2 * BLOCK)[None, None, :]
    mask = (kpos >= 0) & (kpos <= qpos) & (qpos - kpos < WINDOW)
    sink = sinks.astype(jnp.float32).reshape(N_KV_HEADS, GROUP)[None, None, :, :, None, None]
    probs = sink_softmax(scores, mask[None, :, None, None], sink).astype(v.dtype)
    o = jnp.einsum('bnkgqs,bnskd->bnqkgd', probs, vv).reshape(B, T, N_HEADS * HEAD_DIM)
    return o @ w_o, k[:, T - W_BUF:], v[:, T - W_BUF:]


def swa_sample(x, k_buf, v_buf, w_qkv, b_qkv, sinks, w_o):
    B, T, _ = x.shape
    q, k, v = qkv_proj(x, w_qkv, b_qkv, PAST_LEN)
    kk = jnp.concatenate([k_buf.astype(k.dtype), k], axis=1)
    vv = jnp.concatenate([v_buf.astype(v.dtype), v], axis=1)
    qpos = PAST_LEN + jnp.arange(T)
    kpos = PAST_LEN - W_BUF + jnp.arange(W_BUF + T)
    mask = (kpos[None, :] <= qpos[:, None]) & (qpos[:, None] - kpos[None, :] < WINDOW)
    qg = q.reshape(B, T, N_KV_HEADS, GROUP, HEAD_DIM)
    scores = jnp.einsum('btkgd,bskd->bkgts', qg, kk).astype(jnp.float32) * ATTN_SCALE
    sink = sinks.astype(jnp.float32).reshape(N_KV_HEADS, GROUP)[None, :, :, None, None]
    probs = sink_softmax(scores, mask, sink).astype(v.dtype)
    o = jnp.einsum('bkgts,bskd->btkgd', probs, vv).reshape(B, T, N_HEADS * HEAD_DIM)
    return o @ w_o, kk[:, -W_BUF:], vv[:, -W_BUF:]


def gla_recurrence(q, k, v, log_a, s0, chunk):
    B, T, H, DK = q.shape
    DV = v.shape[-1]
    nc = T // chunk

    def to_chunks(t):
        return t.astype(jnp.float32).reshape(B, nc, chunk, H, t.shape[-1]).transpose(1, 0, 3, 2, 4)

    causal = jnp.tril(jnp.ones((chunk, chunk), dtype=bool))

    def step(S, inp):
        qc, kc, vc, lc = inp
        b = jnp.cumsum(lc, axis=2)
        b_last = b[:, :, -1:, :]
        q_in = qc * jnp.exp(b)
        k_in = kc * jnp.exp(-b)
        att = jnp.where(causal, jnp.einsum('bhtd,bhsd->bhts', q_in, k_in), 0.0)
        o = jnp.einsum('bhts,bhsv->bhtv', att, vc) + jnp.einsum('bhtd,bhdv->bhtv', q_in, S)
        S_new = jnp.exp(b_last[:, :, 0, :, None]) * S + jnp.einsum('bhsd,bhsv->bhdv', kc * jnp.exp(b_last - b), vc)
        return S_new, o

    S, o = lax.scan(step, s0.astype(jnp.float32), (to_chunks(q), to_chunks(k), to_chunks(v), to_chunks(log_a)))
    o = o.transpose(1, 0, 3, 2, 4).reshape(B, T, H, DV)
    return o, S


def gla_mix(x, s0, w_in, w_a1, w_a2, b_a, norm_g, w_o):
    B, T, _ = x.shape
    proj = x @ w_in
    q = proj[..., :GLA_DK].reshape(B, T, GLA_HEADS, GLA_DKH) * (GLA_DKH ** -0.5)
    k = proj[..., GLA_DK:2 * GLA_DK].reshape(B, T, GLA_HEADS, GLA_DKH)
    v = proj[..., 2 * GLA_DK:2 * GLA_DK + GLA_DV].reshape(B, T, GLA_HEADS, GLA_DVH)
    g = proj[..., 2 * GLA_DK + GLA_DV:].reshape(B, T, GLA_HEADS, GLA_DVH)
    log_a = jax.nn.log_sigmoid(((x @ w_a1) @ w_a2 + b_a).astype(jnp.float32)) / GLA_TAU
    log_a = log_a.reshape(B, T, GLA_HEADS, GLA_DKH)
    o, s_new = gla_recurrence(q, k, v, log_a, s0, min(GLA_CHUNK, T))
    o = o * lax.rsqrt(jnp.mean(jnp.square(o), -1, keepdims=True) + NORM_EPS) * norm_g.astype(jnp.float32)
    o = (o * jax.nn.silu(g.astype(jnp.float32))).astype(x.dtype).reshape(B, T, GLA_DV)
    return o @ w_o, s_new.astype(s0.dtype)


def conv_ffn(x, buf, w_up, conv_w, conv_b, w_down):
    T = x.shape[1]
    h = x @ w_up
    a, u = h[..., :D_FF], h[..., D_FF:]
    ap = jnp.concatenate([buf.astype(a.dtype), a], axis=1)
    c = conv_b + sum(conv_w[j] * ap[:, j:j + T] for j in range(CONV_W))
    y = (jax.nn.gelu(c, approximate=False) * u) @ w_down
    return y, ap[:, T:]


def setup_inputs(seed: int = 0) -> dict:
    key = jax.random.key(seed)
    ks = jax.random.split(key, 26)

    def nrm(k, shape, scale):
        return jax.random.normal(k, shape, jnp.float32) * scale

    qkv_w = (N_HEADS + 2 * N_KV_HEADS) * HEAD_DIM
    gla_in = 2 * GLA_DK + 2 * GLA_DV
    return {
        'x_prompt': nrm(ks[0], (BATCH, SEQ, D_MODEL), 1.0),
        'x_sample': nrm(ks[1], (DEC_BATCH, DEC_SEQ, D_MODEL), 1.0),
        'cache_k': nrm(ks[2], (N_ATTN, DEC_BATCH, W_BUF, N_KV_HEADS, HEAD_DIM), 1.0),
        'cache_v': nrm(ks[3], (N_ATTN, DEC_BATCH, W_BUF, N_KV_HEADS, HEAD_DIM), 1.0),
        'state_gla': nrm(ks[4], (N_GLA, DEC_BATCH, GLA_HEADS, GLA_DKH, GLA_DVH), 1.0),
        'state_conv': nrm(ks[5], (DEPTH, DEC_BATCH, CONV_W - 1, D_FF), 1.0),
        'attn_w_qkv': nrm(ks[6], (N_ATTN, D_MODEL, qkv_w), D_MODEL ** -0.5),
        'attn_b_qkv': nrm(ks[7], (N_ATTN, qkv_w), 0.02),
        'attn_sinks': nrm(ks[8], (N_ATTN, N_HEADS), 1.0),
        'attn_w_o': nrm(ks[9], (N_ATTN, N_HEADS * HEAD_DIM, D_MODEL), BETA * (N_HEADS * HEAD_DIM) ** -0.5),
        'gla_w_in': nrm(ks[10], (N_GLA, D_MODEL, gla_in), D_MODEL ** -0.5),
        'gla_w_a1': nrm(ks[11], (N_GLA, D_MODEL, GLA_RANK), D_MODEL ** -0.5),
        'gla_w_a2': nrm(ks[12], (N_GLA, GLA_RANK, GLA_DK), GLA_RANK ** -0.5),
        'gla_b_a': nrm(ks[13], (N_GLA, GLA_DK), 0.1),
        'gla_norm_g': 1.0 + nrm(ks[14], (N_GLA, GLA_DVH), 0.02),
        'gla_w_o': nrm(ks[15], (N_GLA, GLA_DV, D_MODEL), BETA * GLA_DV ** -0.5),
        'ffn_w_up': nrm(ks[16], (DEPTH, D_MODEL, 2 * D_FF), D_MODEL ** -0.5),
        'ffn_conv_w': nrm(ks[17], (DEPTH, CONV_W, D_FF), CONV_W ** -0.5),
        'ffn_conv_b': nrm(ks[18], (DEPTH, D_FF), 0.02),
        'ffn_w_down': nrm(ks[19], (DEPTH, D_FF, D_MODEL), BETA * D_FF ** -0.5),
        'ln_mix_g': 1.0 + nrm(ks[20], (DEPTH, D_MODEL), 0.02),
        'ln_mix_b': nrm(ks[21], (DEPTH, D_MODEL), 0.02),
        'ln_ffn_g': 1.0 + nrm(ks[22], (DEPTH, D_MODEL), 0.02),
        'ln_ffn_b': nrm(ks[23], (DEPTH, D_MODEL), 0.02),
    }


def reference(x_prompt, x_sample, cache_k, cache_v, state_gla, state_conv,
              attn_w_qkv, attn_b_qkv, attn_sinks, attn_w_o,
              gla_w_in, gla_w_a1, gla_w_a2, gla_b_a, gla_norm_g, gla_w_o,
              ffn_w_up, ffn_conv_w, ffn_conv_b, ffn_w_down,
              ln_mix_g, ln_mix_b, ln_ffn_g, ln_ffn_b):
    xp, xs = x_prompt, x_sample
    kp_l, vp_l, ks_l, vs_l = [], [], [], []
    sp_l, ss_l, cp_l, cs_l = [], [], [], []
    for i in range(DEPTH):
        j = i // 2
        if i % 2 == 0:
            mp, kp, vp = swa_prompt(xp, attn_w_qkv[j], attn_b_qkv[j], attn_sinks[j], attn_w_o[j])
            ms, k_s, v_s = swa_sample(xs, cache_k[j], cache_v[j], attn_w_qkv[j], attn_b_qkv[j], attn_sinks[j], attn_w_o[j])
            kp_l.append(kp)
            vp_l.append(vp)
            ks_l.append(k_s)
            vs_l.append(v_s)
        else:
            s0 = jnp.zeros((xp.shape[0], GLA_HEADS, GLA_DKH, GLA_DVH), state_gla.dtype)
            mp, sp = gla_mix(xp, s0, gla_w_in[j], gla_w_a1[j], gla_w_a2[j], gla_b_a[j], gla_norm_g[j], gla_w_o[j])
            ms, ss = gla_mix(xs, state_gla[j], gla_w_in[j], gla_w_a1[j], gla_w_a2[j], gla_b_a[j], gla_norm_g[j], gla_w_o[j])
            sp_l.append(sp)
            ss_l.append(ss)
        xp = layer_norm(ALPHA * xp + mp, ln_mix_g[i], ln_mix_b[i])
        xs = layer_norm(ALPHA * xs + ms, ln_mix_g[i], ln_mix_b[i])
        buf0 = jnp.zeros((xp.shape[0], CONV_W - 1, D_FF), state_conv.dtype)
        fp, cp = conv_ffn(xp, buf0, ffn_w_up[i], ffn_conv_w[i], ffn_conv_b[i], ffn_w_down[i])
        fs, cs = conv_ffn(xs, state_conv[i], ffn_w_up[i], ffn_conv_w[i], ffn_conv_b[i], ffn_w_down[i])
        cp_l.append(cp)
        cs_l.append(cs)
        xp = layer_norm(ALPHA * xp + fp, ln_ffn_g[i], ln_ffn_b[i])
        xs = layer_norm(ALPHA * xs + fs, ln_ffn_g[i], ln_ffn_b[i])
    return (xp, xs, jnp.stack(kp_l), jnp.stack(vp_l), jnp.stack(ks_l), jnp.stack(vs_l),
            jnp.stack(sp_l), jnp.stack(ss_l), jnp.stack(cp_l), jnp.stack(cs_l))
```

```python
import os
import numpy as np
from contextlib import ExitStack
import concourse.bass as bass
import concourse.mybir as mybir
from concourse.bass_utils import run_bass_kernel_spmd

F32 = mybir.dt.float32
BF16 = mybir.dt.bfloat16
AF = mybir.ActivationFunctionType
ALU = mybir.AluOpType

NCORES = 8
D = 1024
SEQ = 2048
NSEQ = 2
NSMP = 16
DFF = 2816
NCH = 22
ALPHA = 4.0 ** 0.25
LN_EPS = 1e-5
NORM_EPS = 1e-6
ATT_SCALE = 0.125
DKS = 128.0 ** -0.5
MASKV = -2000.0
RING = 4
SLOTW = NCH * 256


class Res:
    __slots__ = ("name", "w", "rs", "excl", "slot")

    def __init__(self, name, excl=False):
        self.name = name
        self.w = None
        self.rs = {}
        self.excl = excl
        self.slot = None


class Prod:
    def __init__(self, sem):
        self.sem = sem
        self.cnt = 0


class Q(Prod):
    def __init__(self, eng, sem):
        super().__init__(sem)
        self.eng = eng
        self.seen = {}


class T:
    def __init__(self, ap, res):
        self.ap = ap
        self.res = res

    def __getitem__(self, k):
        return self.ap[k]


def V(ap, off, dims, npart=None, pstart=0):
    p = ap.ap[0]
    n = npart if npart is not None else p[1]
    return bass.AP(ap.tensor, ap.offset + pstart * p[0] + off, [[p[0], n]] + [list(d) for d in dims])


class KB:
    def __init__(self):
        self.nc = bass.Bass("TRN2", target_bir_lowering=False)
        self.es = ExitStack()
        nc = self.nc
        self.PE = Q(nc.tensor, self.sem("pe"))
        self.ACT = Q(nc.scalar, self.sem("act"))
        self.DVE = Q(nc.vector, self.sem("dve"))
        self.POOL = Q(nc.gpsimd, self.sem("pool"))
        self.SP = Q(nc.sync, self.sem("sp"))
        self.slots = []
        self.cslot = self.newslot("const")
        self.misc = self.newslot("misc")
        self.nt = 0

    def sem(self, name):
        return self.es.enter_context(self.nc.semaphore(name))

    def newslot(self, name):
        s = Prod(self.sem("d_" + name))
        self.slots.append(s)
        return s

    def sb(self, shape, dt, name=None, track=True):
        self.nt += 1
        nm = name or ("t%d" % self.nt)
        t = self.es.enter_context(self.nc.sbuf_tensor(nm, list(shape), dt))
        return T(t[:], Res(nm) if track else None)

    def _wait(self, q, r, w):
        d = {}

        def add(pr, c):
            if d.get(pr, 0) < c:
                d[pr] = c

        for x in r:
            if x is None:
                continue
            if x.w is not None:
                add(*x.w)
            if x.excl:
                for pr, c in x.rs.items():
                    add(pr, c)
        for x in w:
            if x is None:
                continue
            if x.w is not None:
                add(*x.w)
            for pr, c in x.rs.items():
                add(pr, c)
        for pr, c in d.items():
            if q.seen.get(pr, 0) >= c:
                continue
            q.eng.wait_ge(pr.sem, c)
            q.seen[pr] = c

    def _mark(self, prod, c, r, w):
        for x in r:
            if x is None:
                continue
            if x.excl:
                x.w = (prod, c)
                x.rs = {}
            else:
                x.rs[prod] = c
        for x in w:
            if x is None:
                continue
            x.w = (prod, c)
            x.rs = {}

    def op(self, q, fn, r=(), w=()):
        self._wait(q, r, w)
        ins = fn(q.eng)
        q.cnt += 1
        ins.then_inc(q.sem, 1)
        self._mark(q, q.cnt, r, w)

    def dma(self, q, out, in_, r=(), w=(), slot=None, **kw):
        if slot is None:
            for x in list(w) + list(r):
                if x is not None:
                    if x.slot is None:
                        x.slot = self.newslot(x.name)
                    slot = x.slot
                    break
        if slot is None:
            slot = self.misc
        self._wait(q, r, w)
        ins = q.eng.dma_start(out=out, in_=in_, **kw)
        slot.cnt += 16
        ins.then_inc(slot.sem, 16)
        self._mark(slot, slot.cnt, r, w)


def build(ntiles_per_seq=4, nseq=NSEQ, do_sample=True, nlayers=2):
    kb = KB()
    nc = kb.nc
    PE, ACT, DVE, POOL, SP = kb.PE, kb.ACT, kb.DVE, kb.POOL, kb.SP

    def din(name, shape):
        return nc.dram_tensor(name, list(shape), F32, kind="ExternalInput").ap()

    def dout(name, shape):
        return nc.dram_tensor(name, list(shape), F32, kind="ExternalOutput").ap()

    xp = din("xp", [NSEQ, SEQ, D])
    xs = din("xs", [NSMP, D])
    ck = din("ck", [NSMP, 128, 128])
    cv = din("cv", [NSMP, 128, 128])
    sg = din("sg", [NSMP, 4, 128, 256])
    sc = din("sc", [2, NSMP, 2, DFF])
    wqkv = din("wqkv", [D, 1280])
    bqkv = din("bqkv", [1, 1280])
    sinks = din("sinks", [1, 16])
    woa = din("woa", [D, D])
    win = din("win", [D, 3072])
    wa1 = din("wa1", [D, 16])
    wa2 = din("wa2", [16, 512])
    ba = din("ba", [1, 512])
    ng = din("ng", [1, 256])
    wog = din("wog", [D, D])
    wup = din("wup", [2, D, 2 * DFF])
    cwt = din("cwt", [2, 128, NCH, 4])
    wdn = din("wdn", [2, DFF, D])
    lng = din("lng", [4, D])
    lnb = din("lnb", [4, D])
    c_ident = din("c_ident", [128, 128])
    c_ropec = din("c_ropec", [128, 16, 64])
    c_ropes = din("c_ropes", [128, 16, 64])
    c_ropecs = din("c_ropecs", [NSMP, 64])
    c_ropess = din("c_ropess", [NSMP, 64])
    c_maskp = din("c_maskp", [128, 512])
    c_maskc = din("c_maskc", [128, 512])
    c_trim = din("c_trim", [128, 512])
    c_utri = din("c_utri", [128, 128])
    c_lstr = din("c_lstr", [128, 128])
    c_eye16 = din("c_eye16", [NSMP, NSMP])

    yp = dout("yp", [NSEQ, SEQ, D])
    ys = dout("ys", [NSMP, D])
    nkp = dout("nkp", [NSEQ, 128, 128])
    nvp = dout("nvp", [NSEQ, 128, 128])
    nks = dout("nks", [NSMP, 128, 128])
    nvs = dout("nvs", [NSMP, 128, 128])
    gp = dout("gp", [NSEQ, 4, 128, 256])
    gs = dout("gs", [NSMP, 4, 128, 256])
    cpo = dout("cpo", [2, NSEQ, 2, DFF])
    cso = dout("cso", [2, NSMP, 2, DFF])

    cslot2 = kb.newslot("const_sw")

    def cload(q, t, src):
        ins = q.eng.dma_start(out=t.ap, in_=src)
        sl = cslot2 if q is POOL else kb.cslot
        sl.cnt += 16
        ins.then_inc(sl.sem, 16)

    ident = kb.sb([128, 128], BF16, "ident", False)
    maskp = kb.sb([128, 512], BF16, "maskp", False)
    maskc = kb.sb([128, 512], BF16, "maskc", False)
    trim = kb.sb([128, 512], BF16, "trim", False)
    utri = kb.sb([128, 128], F32, "utri", False)
    lstr = kb.sb([128, 128], F32, "lstr", False)
    ropec = kb.sb([128, 16, 64], F32, "ropec", False)
    ropes = kb.sb([128, 16, 64], F32, "ropes", False)
    ropecs = kb.sb([NSMP, 64], F32, "ropecs", False)
    ropess = kb.sb([NSMP, 64], F32, "ropess", False)
    bq_b = kb.sb([1, 1280], BF16, "bq_b", False)
    ba_b = kb.sb([1, 512], BF16, "ba_b", False)
    wa1_b = kb.sb([128, 8, 16], BF16, "wa1_b", False)
    wa2_b = kb.sb([16, 512], BF16, "wa2_b", False)
    ng_rep = kb.sb([128, 256], F32, "ng_rep", False)
    sink_rep = kb.sb([128, 16], F32, "sink_rep", False)
    cw = [kb.sb([128, NCH, 4], F32, "cw%d" % l, False) for l in range(2)]
    eye16 = kb.sb([NSMP, NSMP], F32, "eye16", False)
    esink = kb.sb([128, 16], F32, "esink")
    ones1 = kb.sb([1, 128], BF16, "ones1")
    ones128 = kb.sb([128, 128], BF16, "ones128")

    cload(POOL, ident, c_ident)
    cload(POOL, maskp, c_maskp)
    cload(POOL, maskc, c_maskc)
    cload(POOL, trim, c_trim)
    cload(SP, utri, c_utri)
    cload(SP, lstr, c_lstr)
    cload(SP, ropec, c_ropec)
    cload(SP, ropes, c_ropes)
    cload(SP, ropecs, c_ropecs)
    cload(SP, ropess, c_ropess)
    cload(POOL, bq_b, bqkv)
    cload(POOL, ba_b, ba)
    cload(POOL, wa1_b, wa1.rearrange("(kc p) r -> p kc r", p=128))
    cload(POOL, wa2_b, wa2)
    cload(SP, ng_rep, ng[0:1, :].partition_broadcast(128))
    cload(SP, sink_rep, sinks[0:1, :].partition_broadcast(128))
    cload(SP, cw[0], cwt[0])
    cload(SP, cw[1], cwt[1])
    cload(SP, eye16, c_eye16)
    ident_f = kb.sb([128, 128], F32, "ident_f", False)
    cload(SP, ident_f, c_ident)
    eye_rep = kb.sb([128, 256], F32, "eye_rep", False)
    baT = kb.sb([128, 4], F32, "baT", False)
    cload(SP, eye_rep, c_eye16.rearrange("a b -> (a b)").rearrange("(o n) -> o n", o=1).partition_broadcast(128))
    with nc.allow_non_contiguous_dma(reason="tiny transposed bias load"):
        cload(SP, baT, ba[0, :].rearrange("(h d) -> d h", d=128))
    for q in (PE, ACT, DVE, POOL, SP):
        q.eng.wait_ge(kb.cslot.sem, kb.cslot.cnt)
        q.eng.wait_ge(cslot2.sem, cslot2.cnt)

    kb.op(ACT, lambda e: e.activation(out=esink.ap, in_=sink_rep.ap, func=AF.Exp), w=[esink.res])
    kb.op(DVE, lambda e: e.memset(ones1.ap, 1.0), w=[ones1.res])
    kb.op(DVE, lambda e: e.memset(ones128.ap, 1.0), w=[ones128.res])

    xtok = [kb.sb([128, D], F32, "xtok%d" % j) for j in range(4)]
    xT = kb.sb([128, 8, 512], BF16, "xT")
    xTr = [Res("xT%d" % j) for j in range(4)]
    gT = kb.sb([128, NCH, 512], BF16, "gT")
    gTres = [Res("gT%d" % c) for c in range(NCH)]
    ring = [kb.sb([128, SLOTW], BF16, "ring%d" % i) for i in range(RING)]
    gbuf = [kb.sb([128, 2, D], F32, "gb%d" % i) for i in range(2)]
    big4a = [kb.sb([128, D], BF16, "big4a%d" % j) for j in range(4)]
    big4b = [kb.sb([128, D], BF16, "big4b%d" % j) for j in range(4)]
    k_r = kb.sb([128, 4, 128], BF16, "k_r")
    KT = kb.sb([128, 8, 128], BF16, "KT")
    KTres = [Res("KT%d" % i) for i in range(8)]
    Vd = kb.sb([128, 8, 2, 128], BF16, "Vd")
    Vdres = [Res("Vd%d" % i) for i in range(8)]
    kf = kb.sb([128, 128], F32, "kf")
    vf = kb.sb([128, 128], F32, "vf")
    qinT = kb.sb([128, 4, 512], BF16, "qinT")
    kinT = kb.sb([128, 4, 512], BF16, "kinT")
    kdec = [kb.sb([128, 512], BF16, "kdec%d" % j) for j in range(4)]
    ebl = kb.sb([128, 4, 4], F32, "ebl")
    Sst = kb.sb([128, 4, 256], F32, "Sst")
    Sbf = kb.sb([128, 4, 256], BF16, "Sbf")
    Sbf2 = [Sbf, kb.sb([128, 4, 256], BF16, "Sbf_b")]
    rT_b = kb.sb([16, 512], BF16, "rT_b")
    aprev = [kb.sb([128, NCH, 2], F32, "aprev%d" % l) for l in range(2)]
    abuf = [kb.sb([128, 514], F32, "abuf%d" % i) for i in range(3)]
    abufh = [Res("abufh%d" % i) for i in range(3)]
    stat = [kb.sb([128, 24], F32, "stat%d" % i) for i in range(2)]

    ksm = kb.sb([NSMP, 128], F32, "ksm")
    vsm = kb.sb([NSMP, 128], F32, "vsm")
    ots = kb.sb([128, 8, NSMP], BF16, "ots")
    gsm = kb.sb([128, NCH, NSMP], BF16, "gsm")
    nbaT = kb.sb([128, 4], F32, "nbaT")
    smallf = kb.sb([128, 4, 64], F32, "smallf")
    ogTs = kb.sb([128, 8, NSMP], BF16, "ogTs")
    hsm = T(V(qinT.ap, 0, [[1, 2048]]).bitcast(F32), qinT.res)
    cst = T(V(kinT.ap, 0, [[1, 2048]]).bitcast(F32), kinT.res)

    class Pool:
        def __init__(self, n, shape, dt, name):
            self.b = [kb.sb(shape, dt, "%s%d" % (name, i)) for i in range(n)]
            self.i = 0

        def next(self):
            t = self.b[self.i % len(self.b)]
            self.i += 1
            return t

    F512 = Pool(6, [128, 512], F32, "f512_")
    rdp = Pool(4, [128, 4], F32, "rd4_")
    kb.op(DVE, lambda e: e.memset(V(Vd.ap, 0, [[1, 2048]]), 1.0), w=Vdres)
    B512 = Pool(5, [128, 512], BF16, "b512_")
    B1024 = Pool(4, [128, 1024], BF16, "b1024_")

    psum = []
    for i in range(8):
        t = kb.es.enter_context(nc.psum_tensor("ps%d" % i, [128, 512], F32))
        psum.append(T(t[:], Res("ps%d" % i, excl=True)))
    psi = [0]

    reserved = set()

    def ps():
        while True:
            i = psi[0] % 8
            psi[0] += 1
            if i not in reserved:
                return psum[i]

    def ps_reserve():
        t = ps()
        reserved.add(psum.index(t))
        return t

    def ps_release(t):
        reserved.discard(psum.index(t))

    def slab_list():
        L = []
        for i, (c0, wd) in enumerate([(0, 512), (512, 512), (1024, 256)]):
            L.append((wqkv[:, c0:c0 + wd], 8, wd))
        for h in range(2):
            L.append((woa[:, h * 512:(h + 1) * 512], 8, 512))
        for s in range(11):
            L.append((wup[0, :, s * 512:(s + 1) * 512], 8, 512))
        for qd in range(4):
            L.append((wdn[0, :, qd * 256:(qd + 1) * 256], NCH, 256))
        if nlayers > 1:
            for i in range(6):
                L.append((win[:, i * 512:(i + 1) * 512], 8, 512))
            for h in range(2):
                L.append((wog[:, h * 512:(h + 1) * 512], 8, 512))
            for s in range(11):
                L.append((wup[1, :, s * 512:(s + 1) * 512], 8, 512))
            for qd in range(4):
                L.append((wdn[1, :, qd * 256:(qd + 1) * 256], NCH, 256))
        return L

    n_passes = nseq * ntiles_per_seq + (1 if do_sample else 0)
    stream = slab_list() * n_passes

    class WS:
        def __init__(self):
            self.pos = 0
            self.rel = 0
            self.issued = 0
            for _ in range(RING):
                self.issue()

        def issue(self):
            if self.issued >= len(stream):
                return
            src, kc, wd = stream[self.issued]
            slot = ring[self.issued % RING]
            dst = V(slot.ap, 0, [[wd, kc], [1, wd]])
            kb.dma(POOL, dst, src.rearrange("(kc p) c -> p kc c", p=128), w=[slot.res])
            self.issued += 1

        def next(self):
            src, kc, wd = stream[self.pos]
            slot = ring[self.pos % RING]
            self.pos += 1
            return T(V(slot.ap, 0, [[wd, kc], [1, wd]]), slot.res)

        def done(self):
            self.rel += 1
            self.issue()

    if nseq * ntiles_per_seq > 0:
        for j in range(4):
            kb.dma(POOL, big4b[j].ap, xp[0, j * 128:(j + 1) * 128, :], w=[big4b[j].res])
    ws = WS()

    def mm_group(out_ap, pairs, r, w, first=True, last=True):
        n = len(pairs)

        def fn(e):
            ins = None
            for i, (l, rr) in enumerate(pairs):
                ins = e.matmul(out_ap, l, rr, start=(first and i == 0), stop=(last and i == n - 1))
            return ins
        kb.op(PE, fn, r=r, w=w)

    def make_xT(j, ntok=128, src_bf=None):
        if src_bf is not None:
            xb = src_bf
        else:
            xb = B1024.next()
            kb.op(ACT, lambda e: e.activation(out=xb.ap[0:ntok, :], in_=xtok[j].ap[0:ntok, :], func=AF.Copy),
                  r=[xtok[j].res], w=[xb.res])
        bank = ps()
        bb = bank.ap.bitcast(BF16)

        def fn(e):
            ins = None
            for kc in range(8):
                ins = e.transpose(bb[:, kc * 128:kc * 128 + ntok], xb.ap[0:ntok, kc * 128:(kc + 1) * 128],
                                  ident.ap[0:ntok, 0:ntok])
            return ins
        kb.op(PE, fn, r=[xb.res], w=[bank.res])
        kb.op(DVE, lambda e: e.tensor_copy(V(xT.ap, j * 128, [[512, 8], [1, ntok]]),
                                           V(bb, 0, [[128, 8], [1, ntok]])),
              r=[bank.res], w=[xTr[j]])

    gb_state = {"i": 0}

    def load_gb(idx):
        g = gbuf[gb_state["i"] % 2]
        gb_state["i"] += 1
        kb.dma(SP, g.ap[:, 0, :], lng[idx:idx + 1, :].partition_broadcast(128), w=[g.res])
        kb.dma(SP, g.ap[:, 1, :], lnb[idx:idx + 1, :].partition_broadcast(128), w=[g.res])
        return g

    def resid_add(j, bank, c0, wd, ntok=128):
        kb.op(DVE, lambda e: e.scalar_tensor_tensor(out=xtok[j].ap[0:ntok, c0:c0 + wd],
                                                    in0=xtok[j].ap[0:ntok, c0:c0 + wd], scalar=ALPHA,
                                                    in1=bank.ap[0:ntok, 0:wd], op0=ALU.mult, op1=ALU.add),
              r=[bank.res, xtok[j].res], w=[xtok[j].res])

    def layer_norm(j, g, ntok=128):
        st = stat[j % 2]
        x = xtok[j]
        for hh in range(2):
            kb.op(DVE, lambda e: e.bn_stats(st.ap[0:ntok, hh * 6:(hh + 1) * 6],
                                            x.ap[0:ntok, hh * 512:(hh + 1) * 512]),
                  r=[x.res], w=[st.res])
        kb.op(DVE, lambda e: e.bn_aggr(st.ap[0:ntok, 12:14], st.ap[0:ntok, 0:12]), r=[st.res], w=[st.res])
        kb.op(ACT, lambda e: e.activation(out=st.ap[0:ntok, 14:15], in_=st.ap[0:ntok, 13:14], func=AF.Ln,
                                          bias=eps_ln.ap[0:ntok, :], scale=1.0), r=[st.res], w=[st.res])
        kb.op(ACT, lambda e: e.activation(out=st.ap[0:ntok, 15:16], in_=st.ap[0:ntok, 14:15], func=AF.Exp,
                                          scale=-0.5), r=[st.res], w=[st.res])
        kb.op(DVE, lambda e: e.scalar_tensor_tensor(out=x.ap[0:ntok, :], in0=x.ap[0:ntok, :],
                                                    scalar=st.ap[0:ntok, 12:13], in1=g.ap[0:ntok, 0, :],
                                                    op0=ALU.subtract, op1=ALU.mult),
              r=[x.res, st.res, g.res], w=[x.res])
        kb.op(DVE, lambda e: e.scalar_tensor_tensor(out=x.ap[0:ntok, :], in0=x.ap[0:ntok, :],
                                                    scalar=st.ap[0:ntok, 15:16], in1=g.ap[0:ntok, 1, :],
                                                    op0=ALU.mult, op1=ALU.add),
              r=[x.res, st.res, g.res], w=[x.res])

    eps_ln = kb.sb([128, 1], F32, "eps_ln")
    eps_nm = kb.sb([128, 1], F32, "eps_nm")
    one_c = kb.sb([128, 1], F32, "one_c")
    kb.op(DVE, lambda e: e.memset(eps_ln.ap, LN_EPS), w=[eps_ln.res])
    kb.op(DVE, lambda e: e.memset(eps_nm.ap, NORM_EPS), w=[eps_nm.res])
    kb.op(DVE, lambda e: e.memset(one_c.ap, 1.0), w=[one_c.res])

    def rope(src, nh, dst, cosap, sinap, ntok=128):
        A = F512.next()
        B = F512.next()
        srcb, res_src = src
        cosb = V(cosap, 0, [[0, nh], [1, 64]], npart=ntok)
        sinb = V(sinap, 0, [[0, nh], [1, 64]], npart=ntok)
        kb.op(DVE, lambda e: e.tensor_tensor(out=V(A.ap, 0, [[64, nh], [1, 64]], npart=ntok),
                                             in0=V(srcb, 0, [[64, nh], [1, 64]], npart=ntok), in1=cosb,
                                             op=ALU.mult), r=[res_src], w=[A.res])
        kb.op(DVE, lambda e: e.tensor_tensor(out=V(B.ap, 0, [[64, nh], [32, 2], [1, 32]], npart=ntok),
                                             in0=V(srcb, 32, [[64, nh], [-32, 2], [1, 32]], npart=ntok),
                                             in1=V(sinap, 0, [[0, nh], [32, 2], [1, 32]], npart=ntok),
                                             op=ALU.mult), r=[res_src], w=[B.res])
        dap, dres = dst
        kb.op(DVE, lambda e: e.tensor_tensor(out=dap, in0=V(A.ap, 0, [[64, nh], [1, 64]], npart=ntok),
                                             in1=V(B.ap, 0, [[64, nh], [1, 64]], npart=ntok), op=ALU.add),
              r=[A.res, B.res], w=dres)

    def attn_prompt(s, t):
        for si in range(2):
            W = ws.next()
            for j in range(4):
                n = 4 * t + j
                bank = ps()
                pairs = [(xT.ap[:, kc, j * 128:(j + 1) * 128], W.ap[:, kc, :]) for kc in range(8)]
                pairs.append((ones1.ap[0:1, :], bq_b.ap[0:1, si * 512:(si + 1) * 512]))
                mm_group(bank.ap, pairs, r=[xTr[j]] + [W.res, ones1.res], w=[bank.res])
                dst = V(big4a[j].ap, si * 64, [[128, 8], [1, 64]])
                rope((bank.ap, bank.res), 8, (dst, [big4a[j].res]), ropec.ap[:, n, :], ropes.ap[:, n, :])
            ws.done()
        W = ws.next()
        for j in range(4):
            n = 4 * t + j
            slot = n % 8
            bank = ps()
            pairs = [(xT.ap[:, kc, j * 128:(j + 1) * 128], W.ap[:, kc, :]) for kc in range(8)]
            pairs.append((ones1.ap[0:1, :], bq_b.ap[0:1, 1024:1280]))
            mm_group(bank.ap[:, 0:256], pairs, r=[xTr[j]] + [W.res, ones1.res], w=[bank.res])
            rope((bank.ap[:, 0:128], bank.res), 2, (V(k_r.ap, j * 128, [[64, 2], [1, 64]]), [k_r.res]),
                 ropec.ap[:, n, :], ropes.ap[:, n, :])
            kb.op(ACT, lambda e: e.activation(out=V(Vd.ap, slot * 256, [[128, 2], [1, 64]]),
                                              in_=V(bank.ap, 128, [[64, 2], [1, 64]]), func=AF.Copy),
                  r=[bank.res], w=[Vdres[slot]])
            if n == 15:
                A = F512.next()
                B = F512.next()
                kb.op(DVE, lambda e: e.tensor_tensor(out=V(A.ap, 0, [[64, 2], [1, 64]]),
                                                     in0=V(bank.ap, 0, [[64, 2], [1, 64]]),
                                                     in1=V(ropec.ap[:, n, :], 0, [[0, 2], [1, 64]]), op=ALU.mult),
                      r=[bank.res], w=[A.res])
                kb.op(DVE, lambda e: e.tensor_tensor(out=V(B.ap, 0, [[64, 2], [32, 2], [1, 32]]),
                                                     in0=V(bank.ap, 32, [[64, 2], [-32, 2], [1, 32]]),
                                                     in1=V(ropes.ap[:, n, :], 0, [[0, 2], [32, 2], [1, 32]]),
                                                     op=ALU.mult), r=[bank.res], w=[B.res])
                kb.op(DVE, lambda e: e.tensor_tensor(out=kf.ap, in0=A.ap[:, 0:128], in1=B.ap[:, 0:128],
                                                     op=ALU.add), r=[A.res, B.res], w=[kf.res])
                kb.op(ACT, lambda e: e.activation(out=vf.ap, in_=bank.ap[:, 128:256], func=AF.Copy),
                      r=[bank.res], w=[vf.res])
                kb.dma(SP, nkp[s], kf.ap, r=[kf.res])
                kb.dma(SP, nvp[s], vf.ap, r=[vf.res])
        ws.done()
        bank = ps()
        bb = bank.ap.bitcast(BF16)

        def fnk(e):
            ins = None
            for j in range(4):
                ins = e.transpose(bb[:, j * 128:(j + 1) * 128], k_r.ap[:, j, :], ident.ap)
            return ins
        kb.op(PE, fnk, r=[k_r.res], w=[bank.res])
        s0 = (4 * t) % 8
        kb.op(ACT, lambda e: e.activation(out=V(KT.ap, s0 * 128, [[1, 512]]), in_=bb[:, 0:512], func=AF.Copy),
              r=[bank.res], w=[KTres[s0 + i] for i in range(4)])

        W0 = ws.next()
        W1 = ws.next()
        g = load_gb(0)
        QTs_, OTs_ = {}, {}

        def prep(j):
            bank = ps()
            bb = bank.ap.bitcast(BF16)

            def fnq(e):
                ins = None
                for gg in range(8):
                    ins = e.transpose(bb[:, gg * 128:(gg + 1) * 128], big4a[j].ap[:, gg * 128:(gg + 1) * 128],
                                      ident.ap)
                return ins
            kb.op(PE, fnq, r=[big4a[j].res], w=[bank.res])
            QT = B1024.next()
            kb.op(ACT, lambda e: e.activation(out=QT.ap, in_=bb, func=AF.Copy), r=[bank.res], w=[QT.res])
            QTs_[j] = QT
            OTs_[j] = B1024.next()

        def stageA(j, kv, half):
            n = 4 * t + j
            slot = n % 8
            pslot = (n - 1) % 8
            has_prev = n > 0
            QT = QTs_[j]
            pl = slice(kv * 64, (kv + 1) * 64)
            qs = QT.ap[pl, half * 512:(half + 1) * 512]
            Pp = None
            if has_prev:
                bSp = ps()
                mm_group(bSp.ap, [(V(KT.ap, pslot * 128, [[1, 128]], npart=64, pstart=kv * 64), qs)],
                         r=[KTres[pslot], QT.res], w=[bSp.res])
                Pp = B512.next()
                kb.op(ACT, lambda e: e.activation(out=Pp.ap, in_=bSp.ap, func=AF.Exp, scale=ATT_SCALE),
                      r=[bSp.res], w=[Pp.res])
                kb.op(DVE, lambda e: e.tensor_tensor(out=Pp.ap, in0=Pp.ap, in1=maskp.ap, op=ALU.mult),
                      r=[Pp.res], w=[Pp.res])
            bSc = ps()
            mm_group(bSc.ap, [(V(KT.ap, slot * 128, [[1, 128]], npart=64, pstart=kv * 64), qs)],
                     r=[KTres[slot], QT.res], w=[bSc.res])
            Pc = B512.next()
            kb.op(ACT, lambda e: e.activation(out=Pc.ap, in_=bSc.ap, func=AF.Exp, scale=ATT_SCALE),
                  r=[bSc.res], w=[Pc.res])
            kb.op(DVE, lambda e: e.tensor_tensor(out=Pc.ap, in0=Pc.ap, in1=maskc.ap, op=ALU.mult),
                  r=[Pc.res], w=[Pc.res])
            return (Pp, Pc)

        def stageB(j, kv, half, PP):
            n = 4 * t + j
            slot = n % 8
            pslot = (n - 1) % 8
            has_prev = n > 0
            Pp, Pc = PP
            Otok = OTs_[j]
            bO = ps()
            vcur = V(Vd.ap, slot * 256 + kv * 128, [[1, 65]])
            vprev = V(Vd.ap, pslot * 256 + kv * 128, [[1, 65]])

            def fnpv(e):
                ins = None
                for gg in range(4):
                    o_ap = bO.ap[:, gg * 128:gg * 128 + 65]
                    if has_prev:
                        ins = e.matmul(o_ap, Pp.ap[:, gg * 128:(gg + 1) * 128], vprev, start=True, stop=False)
                    ins = e.matmul(o_ap, Pc.ap[:, gg * 128:(gg + 1) * 128], vcur, start=(not has_prev), stop=True)
                return ins
            rr = [Vdres[slot], Pc.res] + ([Vdres[pslot], Pp.res] if has_prev else [])
            kb.op(PE, fnpv, r=rr, w=[bO.res])
            rd4 = rdp.next()
            h0 = kv * 8 + half * 4
            kb.op(DVE, lambda e: e.tensor_tensor(out=rd4.ap, in0=V(bO.ap, 64, [[128, 4]]),
                                                 in1=esink.ap[:, h0:h0 + 4], op=ALU.add),
                  r=[bO.res, esink.res], w=[rd4.res])
            kb.op(DVE, lambda e: e.reciprocal(rd4.ap, rd4.ap), r=[rd4.res], w=[rd4.res])
            kb.op(DVE, lambda e: e.tensor_tensor(out=V(Otok.ap, h0 * 64, [[64, 4], [1, 64]]),
                                                 in0=V(bO.ap, 0, [[128, 4], [1, 64]]),
                                                 in1=V(rd4.ap, 0, [[1, 4], [0, 64]]), op=ALU.mult),
                  r=[bO.res, rd4.res], w=[Otok.res])

        def post(j):
            Otok = OTs_[j]
            bank = ps()
            bb = bank.ap.bitcast(BF16)

            def fnt(e):
                ins = None
                for c in range(8):
                    ins = e.transpose(bb[:, c * 128:(c + 1) * 128], Otok.ap[:, c * 128:(c + 1) * 128], ident.ap)
                return ins
            kb.op(PE, fnt, r=[Otok.res], w=[bank.res])
            OT = B1024.next()
            kb.op(ACT, lambda e: e.activation(out=OT.ap, in_=bb, func=AF.Copy), r=[bank.res], w=[OT.res])
            for half, W in enumerate((W0, W1)):
                bank = ps()
                mm_group(bank.ap, [(OT.ap[:, c * 128:(c + 1) * 128], W.ap[:, c, :]) for c in range(8)],
                         r=[OT.res, W.res], w=[bank.res])
                resid_add(j, bank, half * 512, 512)
            layer_norm(j, g)

        units = [(j, kv, half) for j in range(4) for kv in range(2) for half in range(2)]
        prevu = None
        deferred = []
        for (j, kv, half) in units:
            if kv == 0 and half == 0:
                prep(j)
            PP = stageA(j, kv, half)
            if prevu is not None:
                pj, pkv, phalf, pPP = prevu
                stageB(pj, pkv, phalf, pPP)
                if deferred and pkv == 1 and phalf == 0:
                    make_xT(deferred.pop(0))
                if pkv == 1 and phalf == 1:
                    post(pj)
                    deferred.append(pj)
            prevu = (j, kv, half, PP)
        pj, pkv, phalf, pPP = prevu
        stageB(pj, pkv, phalf, pPP)
        while deferred:
            make_xT(deferred.pop(0))
        post(pj)
        make_xT(pj)
        ws.done()
        ws.done()

    def ffn_prompt(l, s, t, ln_idx, after=None, mid=None):
        cwl = cw[l]
        ap_ = aprev[l]
        if t == 0:
            kb.op(DVE, lambda e: e.memset(ap_.ap, 0.0), w=[ap_.res])
        g = load_gb(ln_idx)
        pend = []
        for si in range(11):
            W = ws.next()
            for ci in range(4):
                cc = si * 4 + ci
                bank = ps()
                mm_group(bank.ap, [(W.ap[:, kc, ci * 128:(ci + 1) * 128], xT.ap[:, kc, :]) for kc in range(8)],
                         r=xTr + [W.res], w=[bank.res])
                if cc < NCH:
                    c = cc
                    ab = abuf[c % 3]
                    abh = abufh[c % 3]
                    kb.op(ACT, lambda e: e.activation(out=ab.ap[:, 0:2], in_=ap_.ap[:, c, :], func=AF.Copy),
                          r=[ap_.res], w=[abh])
                    kb.op(ACT, lambda e: e.activation(out=ab.ap[:, 2:514], in_=bank.ap, func=AF.Copy),
                          r=[bank.res], w=[ab.res])
                    if len(pend) > 1:
                        pend.pop(0)()
                    c1 = F512.next()
                    c2 = F512.next()
                    kb.op(POOL, lambda e: e.tensor_scalar(c1.ap, ab.ap[:, 0:512], cwl.ap[:, c, 0:1],
                                                          cwl.ap[:, c, 3:4], op0=ALU.mult, op1=ALU.add),
                          r=[ab.res, abh], w=[c1.res])
                    kb.op(DVE, lambda e: e.scalar_tensor_tensor(out=c2.ap, in0=ab.ap[:, 1:513],
                                                                scalar=cwl.ap[:, c, 1:2], in1=c1.ap,
                                                                op0=ALU.mult, op1=ALU.add),
                          r=[ab.res, abh, c1.res], w=[c2.res])
                    kb.op(DVE, lambda e: e.scalar_tensor_tensor(out=c1.ap, in0=ab.ap[:, 2:514],
                                                                scalar=cwl.ap[:, c, 2:3], in1=c2.ap,
                                                                op0=ALU.mult, op1=ALU.add),
                          r=[ab.res, c2.res], w=[c1.res])
                    kb.op(ACT, lambda e: e.activation(out=ap_.ap[:, c, :], in_=ab.ap[:, 512:514], func=AF.Copy),
                          r=[ab.res], w=[ap_.res])

                    def gelu_later(c=c, c1=c1):
                        kb.op(ACT, lambda e: e.activation(out=gT.ap[:, c, :], in_=c1.ap, func=AF.Gelu),
                              r=[c1.res], w=[gTres[c]])
                    pend.append(gelu_later)
                else:
                    while pend:
                        pend.pop(0)()
                    c = cc - NCH
                    kb.op(DVE, lambda e: e.tensor_tensor(out=gT.ap[:, c, :], in0=gT.ap[:, c, :], in1=bank.ap,
                                                         op=ALU.mult),
                          r=[bank.res, gTres[c]], w=[gTres[c]])
            ws.done()
        if t == 3:
            for jj in range(2):
                kb.dma(SP, cpo[l, s, jj, :].rearrange("(c p o) -> p c o", p=128, o=1),
                       V(ap_.ap, jj, [[2, NCH], [1, 1]]), r=[ap_.res], allow_slow_non_contiguous=True)
        Wd = [ws.next() for _ in range(4)]
        for j in range(4):
            if j == 2 and mid is not None:
                mid()
            for qd in range(4):
                bank = ps()
                mm_group(bank.ap[:, 0:256],
                         [(gT.ap[:, c, j * 128:(j + 1) * 128], Wd[qd].ap[:, c, :]) for c in range(NCH)],
                         r=gTres + [Wd[qd].res], w=[bank.res])
                resid_add(j, bank, qd * 256, 256)
            layer_norm(j, g)
            if j >= 1 and after is not None:
                after(j - 1)
        for _ in range(4):
            ws.done()
        if after is not None:
            after(3)

    def gla_prompt(s, t):
        Wq = ws.next()
        Wk = ws.next()
        if t == 0:
            kb.op(DVE, lambda e: e.memset(Sst.ap, 0.0), w=[Sst.res])
        for j in range(4):
            js = slice(j * 128, (j + 1) * 128)
            bank = ps()
            mm_group(bank.ap[0:16, 0:128], [(wa1_b.ap[:, kc, :], xT.ap[:, kc, js]) for kc in range(8)],
                     r=[xTr[j]], w=[bank.res])
            kb.op(ACT, lambda e: e.activation(out=rT_b.ap[:, js], in_=bank.ap[0:16, 0:128], func=AF.Copy),
                  r=[bank.res], w=[rT_b.res])
            bz = ps()
            mm_group(bz.ap, [(rT_b.ap[:, js], wa2_b.ap), (ones1.ap[0:1, :], ba_b.ap[0:1, :])],
                     r=[rT_b.res, ones1.res], w=[bz.res])
            e1 = F512.next()
            sp = F512.next()
            kb.op(ACT, lambda e: e.activation(out=e1.ap, in_=bz.ap, func=AF.Exp, scale=-1.0),
                  r=[bz.res], w=[e1.res])
            kb.op(ACT, lambda e: e.activation(out=sp.ap, in_=e1.ap, func=AF.Ln, bias=one_c.ap, scale=1.0),
                  r=[e1.res, one_c.res], w=[sp.res])
            bb_ = ps()

            def fnb(e, sp=sp, bb_=bb_):
                ins = None
                for h in range(4):
                    ins = e.matmul(bb_.ap[:, h * 128:(h + 1) * 128], sp.ap[:, h * 128:(h + 1) * 128], utri.ap,
                                   start=True, stop=True)
                return ins
            kb.op(PE, fnb, r=[sp.res], w=[bb_.res])
            bd = ps()
            mm_group(bd.ap, [(lstr.ap, sp.ap)], r=[sp.res], w=[bd.res])
            eb = F512.next()
            enb = F512.next()
            dec = F512.next()
            kb.op(ACT, lambda e: e.activation(out=eb.ap, in_=bb_.ap, func=AF.Exp), r=[bb_.res], w=[eb.res])
            kb.op(ACT, lambda e: e.activation(out=enb.ap, in_=bb_.ap, func=AF.Exp, scale=-1.0),
                  r=[bb_.res], w=[enb.res])
            kb.op(ACT, lambda e: e.activation(out=dec.ap, in_=bd.ap, func=AF.Exp), r=[bd.res], w=[dec.res])
            kb.op(DVE, lambda e: e.tensor_copy(ebl.ap[:, j, :], V(eb.ap, 127, [[128, 4]])),
                  r=[eb.res], w=[ebl.res])
            for (Wx, dstT, mult) in ((Wq, qinT, eb), (Wk, kinT, enb)):
                bq = ps()

                def fnq(e, Wx=Wx, bq=bq, js=js):
                    ins = None
                    for h in range(4):
                        for kc in range(8):
                            ins = e.matmul(bq.ap[:, h * 128:(h + 1) * 128], Wx.ap[:, kc, h * 128:(h + 1) * 128],
                                           xT.ap[:, kc, js], start=(kc == 0), stop=(kc == 7))
                    return ins
                kb.op(PE, fnq, r=[xTr[j]] + [Wx.res], w=[bq.res])
                sc_ = DKS if dstT is qinT else 1.0
                kb.op(DVE, lambda e: e.scalar_tensor_tensor(out=V(dstT.ap, j * 128, [[512, 4], [1, 128]]),
                                                            in0=V(bq.ap, 0, [[128, 4], [1, 128]]), scalar=sc_,
                                                            in1=V(mult.ap, 0, [[128, 4], [1, 128]]),
                                                            op0=ALU.mult, op1=ALU.mult),
                      r=[bq.res, mult.res], w=[dstT.res])
            bkt = ps()
            mm_group(bkt.ap, [(xT.ap[:, kc, js], Wk.ap[:, kc, :]) for kc in range(8)],
                     r=[xTr[j]] + [Wk.res], w=[bkt.res])
            kb.op(DVE, lambda e: e.tensor_tensor(out=kdec[j].ap, in0=bkt.ap, in1=dec.ap, op=ALU.mult),
                  r=[bkt.res, dec.res], w=[kdec[j].res])
        ws.done()
        ws.done()
        for vi in range(2):
            W = ws.next()
            for j in range(4):
                bank = ps()
                mm_group(bank.ap, [(xT.ap[:, kc, j * 128:(j + 1) * 128], W.ap[:, kc, :]) for kc in range(8)],
                         r=[xTr[j]] + [W.res], w=[bank.res])
                kb.op(ACT, lambda e: e.activation(out=big4a[j].ap[:, vi * 512:(vi + 1) * 512], in_=bank.ap,
                                                  func=AF.Copy), r=[bank.res], w=[big4a[j].res])
            ws.done()
        for gi in range(2):
            W = ws.next()
            for j in range(4):
                bank = ps()
                mm_group(bank.ap, [(xT.ap[:, kc, j * 128:(j + 1) * 128], W.ap[:, kc, :]) for kc in range(8)],
                         r=[xTr[j]] + [W.res], w=[bank.res])
                kb.op(ACT, lambda e: e.activation(out=big4b[j].ap[:, gi * 512:(gi + 1) * 512], in_=bank.ap,
                                                  func=AF.Silu), r=[bank.res], w=[big4b[j].res])
            ws.done()
        W0 = ws.next()
        W1 = ws.next()
        g = load_gb(2)
        OGB = {}

        ATT, BOS = {}, {}

        def sa(j):
            n = 4 * t + j
            vt = big4a[j]
            for hp in range(2):
                bU = ps()

                def fnu(e, bU=bU, hp=hp, vt=vt, j=j):
                    ins = None
                    for h in (2 * hp, 2 * hp + 1):
                        ins = e.matmul(bU.ap[:, (h % 2) * 256:(h % 2 + 1) * 256],
                                       kdec[j].ap[:, h * 128:(h + 1) * 128], vt.ap[:, h * 256:(h + 1) * 256],
                                       start=True, stop=True)
                    return ins
                kb.op(PE, fnu, r=[kdec[j].res, vt.res], w=[bU.res])
                for h in (2 * hp, 2 * hp + 1):
                    kb.op(DVE, lambda e: e.scalar_tensor_tensor(
                        out=Sst.ap[:, h, :], in0=Sst.ap[:, h, :], scalar=ebl.ap[:, j, h:h + 1],
                        in1=bU.ap[:, (h % 2) * 256:(h % 2 + 1) * 256], op0=ALU.mult, op1=ALU.add),
                        r=[bU.res, Sst.res, ebl.res], w=[Sst.res])
            if n == 15:
                kb.dma(SP, gp[s].rearrange("h d v -> d h v"), Sst.ap, r=[Sst.res])
            else:
                sb_ = Sbf2[n % 2]
                kb.op(ACT, lambda e: e.activation(out=sb_.ap, in_=Sst.ap, func=AF.Copy),
                      r=[Sst.res], w=[sb_.res])

        def sb(j):
            js = slice(j * 128, (j + 1) * 128)
            bA = ps()

            def fna(e, bA=bA, js=js):
                ins = None
                for h in range(4):
                    ins = e.matmul(bA.ap[:, h * 128:(h + 1) * 128], kinT.ap[:, h, js], qinT.ap[:, h, js],
                                   start=True, stop=True)
                return ins
            kb.op(PE, fna, r=[kinT.res, qinT.res], w=[bA.res])
            att = B512.next()
            kb.op(DVE, lambda e: e.tensor_tensor(out=att.ap, in0=bA.ap, in1=trim.ap, op=ALU.mult),
                  r=[bA.res], w=[att.res])
            ATT[j] = att

        def sc1(j):
            n = 4 * t + j
            first = n == 0
            js = slice(j * 128, (j + 1) * 128)
            vt = big4a[j]
            att = ATT[j]
            sprev = Sbf2[(n - 1) % 2]
            st = stat[j % 2]
            kb.op(DVE, lambda e: e.memset(st.ap[:, 16:20], 0.0), w=[st.res])
            bOs = []
            for hp in range(2):
                bO = ps()
                bOs.append(bO)

                def fno(e, bO=bO, hp=hp, att=att, vt=vt, js=js, first=first):
                    ins = None
                    for h in (2 * hp, 2 * hp + 1):
                        o_ap = bO.ap[:, (h % 2) * 256:(h % 2 + 1) * 256]
                        ins = e.matmul(o_ap, att.ap[:, h * 128:(h + 1) * 128], vt.ap[:, h * 256:(h + 1) * 256],
                                       start=True, stop=first)
                        if not first:
                            ins = e.matmul(o_ap, qinT.ap[:, h, js], sprev.ap[:, h, :], start=False, stop=True)
                    return ins
                kb.op(PE, fno, r=[att.res, vt.res, qinT.res, sprev.res], w=[bO.res])
                for h in (2 * hp, 2 * hp + 1):
                    junk = F512.next()
                    kb.op(ACT, lambda e: e.activation(out=junk.ap[:, 0:256],
                                                      in_=bO.ap[:, (h % 2) * 256:(h % 2 + 1) * 256],
                                                      func=AF.Square, accum_out=st.ap[:, 16 + h:17 + h]),
                          r=[bO.res], w=[junk.res, st.res])
            kb.op(ACT, lambda e: e.activation(out=st.ap[:, 20:24], in_=st.ap[:, 16:20], func=AF.Ln,
                                              bias=eps_nm.ap, scale=1.0 / 256.0), r=[st.res, eps_nm.res],
                  w=[st.res])
            kb.op(ACT, lambda e: e.activation(out=st.ap[:, 20:24], in_=st.ap[:, 20:24], func=AF.Exp, scale=-0.5),
                  r=[st.res], w=[st.res])
            BOS[j] = bOs

        def sc2(j):
            st = stat[j % 2]
            bOs = BOS[j]
            ogb = B1024.next()
            for hp in range(2):
                ogt = F512.next()
                for h in (2 * hp, 2 * hp + 1):
                    kb.op(DVE, lambda e: e.scalar_tensor_tensor(
                        out=ogt.ap[:, (h % 2) * 256:(h % 2 + 1) * 256],
                        in0=bOs[hp].ap[:, (h % 2) * 256:(h % 2 + 1) * 256], scalar=st.ap[:, 20 + h:21 + h],
                        in1=ng_rep.ap, op0=ALU.mult, op1=ALU.mult), r=[bOs[hp].res, st.res], w=[ogt.res])
                kb.op(DVE, lambda e: e.tensor_tensor(out=ogb.ap[:, hp * 512:(hp + 1) * 512], in0=ogt.ap,
                                                     in1=big4b[j].ap[:, hp * 512:(hp + 1) * 512], op=ALU.mult),
                      r=[ogt.res, big4b[j].res], w=[ogb.res])
            OGB[j] = ogb

        def s2(j):
            ogb = OGB[j]
            bank = ps()
            bb = bank.ap.bitcast(BF16)

            def fnt(e, bb=bb, ogb=ogb):
                ins = None
                for c in range(8):
                    ins = e.transpose(bb[:, c * 128:(c + 1) * 128], ogb.ap[:, c * 128:(c + 1) * 128], ident.ap)
                return ins
            kb.op(PE, fnt, r=[ogb.res], w=[bank.res])
            ogT = B1024.next()
            kb.op(ACT, lambda e: e.activation(out=ogT.ap, in_=bb, func=AF.Copy), r=[bank.res], w=[ogT.res])
            for half, W in enumerate((W0, W1)):
                bank = ps()
                mm_group(bank.ap, [(ogT.ap[:, c * 128:(c + 1) * 128], W.ap[:, c, :]) for c in range(8)],
                         r=[ogT.res, W.res], w=[bank.res])
                resid_add(j, bank, half * 512, 512)
            layer_norm(j, g)

        for f_, a_ in ((sa, 0), (sb, 0), (sc1, 0), (sa, 1), (sb, 1), (sc2, 0), (sc1, 1), (sa, 2), (sb, 2),
                       (sc2, 1), (s2, 0), (sc1, 2), (sa, 3), (sb, 3), (sc2, 2), (s2, 1), (make_xT, 0),
                       (sc1, 3), (sc2, 3), (s2, 2), (make_xT, 1), (s2, 3), (make_xT, 2), (make_xT, 3)):
            f_(a_)
        ws.done()
        ws.done()

    NS = NSMP

    SLV = float(os.environ.get("MK_SLV", "99"))

    def attn_sample():
        g = load_gb(0)
        if SLV < 1:
            return
        KV_sb = [(xtok[1], xtok[2]), (xtok[3], T(V(Sst.ap, 0, [[1, 1024]]), Sst.res))]
        for g8 in range(2):
            bsl = slice(8 * g8, 8 * g8 + 8)
            for (src_c, sb_) in ((ck, KV_sb[g8][0]), (cv, KV_sb[g8][1])):
                sl_ = kb.newslot("kvsw_" + sb_.res.name)
                kb.dma(POOL, V(sb_.ap, 0, [[128, 8], [1, 128]], npart=64),
                       src_c[bsl, 1:65, :].rearrange("b s c -> s b c"), w=[sb_.res], slot=sl_)
                kb.dma(POOL, V(sb_.ap, 0, [[128, 8], [1, 128]], npart=63, pstart=64),
                       src_c[bsl, 65:128, :].rearrange("b s c -> s b c"), w=[sb_.res], slot=sl_)
        for si in range(2):
            W = ws.next()
            bank = ps()
            pairs = [(xT.ap[:, kc, 0:NS], W.ap[:, kc, :]) for kc in range(8)]
            pairs.append((ones1.ap[0:1, 0:NS], bq_b.ap[0:1, si * 512:(si + 1) * 512]))
            mm_group(bank.ap[0:NS, :], pairs, r=[xTr[0]] + [W.res, ones1.res], w=[bank.res])
            dst = V(big4a[0].ap, si * 64, [[128, 8], [1, 64]], npart=NS)
            rope((bank.ap[0:NS, :], bank.res), 8, (dst, [big4a[0].res]), ropecs.ap, ropess.ap, ntok=NS)
            ws.done()
        W = ws.next()
        bank = ps()
        pairs = [(xT.ap[:, kc, 0:NS], W.ap[:, kc, :]) for kc in range(8)]
        pairs.append((ones1.ap[0:1, 0:NS], bq_b.ap[0:1, 1024:1280]))
        mm_group(bank.ap[0:NS, 0:256], pairs, r=[xTr[0]] + [W.res, ones1.res], w=[bank.res])
        rope((bank.ap[0:NS, 0:128], bank.res), 2, (V(ksm.ap, 0, [[64, 2], [1, 64]]), [ksm.res]),
             ropecs.ap, ropess.ap, ntok=NS)
        kb.op(ACT, lambda e: e.activation(out=vsm.ap, in_=bank.ap[0:NS, 128:256], func=AF.Copy),
              r=[bank.res], w=[vsm.res])
        ws.done()
        if SLV < 2:
            return
        bank = ps()
        bb = bank.ap.bitcast(BF16)

        def fnq(e):
            ins = None
            for gg in range(8):
                ins = e.transpose(bb[:, gg * NS:(gg + 1) * NS], big4a[0].ap[0:NS, gg * 128:(gg + 1) * 128],
                                  ident.ap[0:NS, 0:NS])
            return ins
        kb.op(PE, fnq, r=[big4a[0].res], w=[bank.res])
        QTs = B512.next()
        kb.op(DVE, lambda e: e.memset(QTs.ap[:, 0:256], 0.0), w=[QTs.res])
        for kv in range(2):
            kb.op(ACT, lambda e: e.activation(out=V(QTs.ap, kv * 8, [[1, 8], [16, NS]], npart=64, pstart=kv * 64),
                                              in_=V(bb, 0, [[NS, 8], [1, NS]], npart=64, pstart=kv * 64),
                                              func=AF.Copy), r=[bank.res], w=[QTs.res])
        W0 = ws.next()
        W1 = ws.next()
        for g8 in range(2):
            if SLV < 3:
                continue
            bsl = slice(8 * g8, 8 * g8 + 8)
            Ksb, Vsb = KV_sb[g8]
            for (src_c, sb_, sm, outd) in ((ck, Ksb, ksm, nks), (cv, Vsb, vsm, nvs)):
                kb.dma(SP, V(sb_.ap, 0, [[128, 8], [1, 128]], npart=1, pstart=127), sm.ap[bsl, :],
                       r=[sm.res], w=[sb_.res])
                kb.dma(SP, outd[bsl].rearrange("b s c -> s b c"), V(sb_.ap, 0, [[128, 8], [1, 128]]),
                       r=[sb_.res])
            if SLV < 4:
                continue
            Kb = B1024.next()
            Vdk = [B1024.next(), B1024.next()]
            kb.op(ACT, lambda e: e.activation(out=Kb.ap, in_=Ksb.ap, func=AF.Copy), r=[Ksb.res], w=[Kb.res])
            for kv in range(2):
                kb.op(ACT, lambda e: e.activation(out=V(Vdk[kv].ap, 0, [[128, 8], [64, 2], [1, 64]]),
                                                  in_=V(Vsb.ap, kv * 64, [[128, 8], [0, 2], [1, 64]]),
                                                  func=AF.Copy), r=[Vsb.res], w=[Vdk[kv].res])
            bank = ps()
            bb = bank.ap.bitcast(BF16)

            def fnk(e, bb=bb, Kb=Kb):
                ins = None
                for bl in range(8):
                    ins = e.transpose(bb[:, bl * 128:(bl + 1) * 128], Kb.ap[:, bl * 128:(bl + 1) * 128], ident.ap)
                return ins
            kb.op(PE, fnk, r=[Kb.res], w=[bank.res])
            KTs = B1024.next()
            kb.op(ACT, lambda e: e.activation(out=KTs.ap, in_=bb, func=AF.Copy), r=[bank.res], w=[KTs.res])
            if SLV < 4.2:
                continue
            bS = ps()

            def fns(e, bS=bS, KTs=KTs, g8=g8):
                ins = None
                for bl in range(8):
                    b = 8 * g8 + bl
                    ins = e.matmul(bS.ap[:, bl * 16:(bl + 1) * 16], KTs.ap[:, bl * 128:(bl + 1) * 128],
                                   QTs.ap[:, b * 16:(b + 1) * 16], start=True, stop=True)
                return ins
            kb.op(PE, fns, r=[KTs.res, QTs.res], w=[bS.res])
            Ps = B512.next()
            kb.op(ACT, lambda e: e.activation(out=Ps.ap[:, 0:128], in_=bS.ap[:, 0:128], func=AF.Exp,
                                              scale=ATT_SCALE), r=[bS.res], w=[Ps.res])
            if SLV < 4.3:
                continue
            bO = ps()

            def fno(e, bO=bO, Vdk=Vdk, Ps=Ps):
                ins = None
                for bl in range(8):
                    for kv in range(2):
                        co = bl * 16 + kv * 8
                        ins = e.matmul(bO.ap[:, co:co + 8], Vdk[kv].ap[:, bl * 128:(bl + 1) * 128],
                                       Ps.ap[:, co:co + 8], start=True, stop=True)
                return ins
            kb.op(PE, fno, r=[Vdk[0].res, Vdk[1].res, Ps.res], w=[bO.res])
            bD = ps()
            mm_group(bD.ap[:, 0:128], [(ones128.ap, Ps.ap[:, 0:128])], r=[ones128.res, Ps.res], w=[bD.res])
            if SLV < 4.4:
                continue
            rd = F512.next()
            kb.op(DVE, lambda e: e.tensor_tensor(out=V(rd.ap, 0, [[16, 8], [1, 16]]),
                                                 in0=V(bD.ap, 0, [[16, 8], [1, 16]]),
                                                 in1=V(esink.ap, 0, [[0, 8], [1, 16]]), op=ALU.add),
                  r=[bD.res, esink.res], w=[rd.res])
            kb.op(DVE, lambda e: e.reciprocal(rd.ap[:, 0:128], rd.ap[:, 0:128]), r=[rd.res], w=[rd.res])
            if SLV < 4.5:
                continue
            for kv in range(2):
                for par in range(2):
                    kb.op(DVE, lambda e: e.tensor_tensor(
                        out=V(ots.ap, kv * 64 + g8 * 8, [[16, 4], [1, 8]], npart=64, pstart=par * 64),
                        in0=V(bO.ap, kv * 8 + par, [[2, 4], [16, 8]], npart=64, pstart=par * 64),
                        in1=V(rd.ap, kv * 8 + par, [[2, 4], [16, 8]], npart=64, pstart=par * 64),
                        op=ALU.mult), r=[bO.res, rd.res], w=[ots.res])
        if SLV < 5:
            return
        for half, W in enumerate((W0, W1)):
            bank = ps()
            mm_group(bank.ap[0:NS, :], [(ots.ap[:, c, :], W.ap[:, c, :]) for c in range(8)],
                     r=[ots.res, W.res], w=[bank.res])
            resid_add(0, bank, half * 512, 512, ntok=NS)
        layer_norm(0, g, NS)
        make_xT(0, NS)
        ws.done()
        ws.done()

    def ffn_sample(l, ln_idx):
        cwl = cw[l]
        g = load_gb(ln_idx)
        kb.dma(SP, cso[l, :, 0, :], sc[l, :, 1, :])
        for jj in range(2):
            for p in range(6):
                wd = min(512, DFF - p * 512)
                nchk = wd // 128
                tb = F512.next()
                kb.dma(SP, tb.ap[0:NS, 0:wd], sc[l, :, jj, p * 512:p * 512 + wd], w=[tb.res])
                bank = ps()

                def fntr(e, tb=tb, bank=bank, nchk=nchk):
                    ins = None
                    for ci in range(nchk):
                        ins = e.transpose(bank.ap[:, ci * NS:(ci + 1) * NS], tb.ap[0:NS, ci * 128:(ci + 1) * 128],
                                          ident_f.ap[0:NS, 0:NS])
                    return ins
                kb.op(PE, fntr, r=[tb.res], w=[bank.res])
                kb.op(ACT, lambda e: e.activation(out=V(cst.ap, p * 4 * 32 + jj * 16, [[32, nchk], [1, NS]]),
                                                  in_=V(bank.ap, 0, [[NS, nchk], [1, NS]]), func=AF.Copy),
                      r=[bank.res], w=[cst.res])
        for si in range(11):
            W = ws.next()
            bank = ps()

            def fnu(e, W=W, bank=bank):
                ins = None
                for ci in range(4):
                    for kc in range(8):
                        ins = e.matmul(bank.ap[:, ci * NS:(ci + 1) * NS], W.ap[:, kc, ci * 128:(ci + 1) * 128],
                                       xT.ap[:, kc, 0:NS], start=(kc == 0), stop=(kc == 7))
                return ins
            kb.op(PE, fnu, r=xTr + [W.res], w=[bank.res])
            kb.op(ACT, lambda e: e.activation(out=hsm.ap[:, si * 64:(si + 1) * 64], in_=bank.ap[:, 0:64],
                                              func=AF.Copy), r=[bank.res], w=[hsm.res])
            ws.done()
        NA = NCH * NS
        t1 = F512.next()
        t2 = F512.next()
        wv = lambda k: V(cwl.ap, k, [[4, NCH], [0, NS]])
        v3 = lambda ap, off=0: V(ap, off, [[NS, NCH], [1, NS]])
        kb.op(DVE, lambda e: e.tensor_tensor(out=v3(t1.ap), in0=V(cst.ap, 0, [[32, NCH], [1, NS]]), in1=wv(0),
                                             op=ALU.mult), r=[cst.res], w=[t1.res])
        kb.op(DVE, lambda e: e.tensor_tensor(out=v3(t2.ap), in0=V(cst.ap, 16, [[32, NCH], [1, NS]]), in1=wv(1),
                                             op=ALU.mult), r=[cst.res], w=[t2.res])
        kb.op(DVE, lambda e: e.tensor_tensor(out=t1.ap[:, 0:NA], in0=t1.ap[:, 0:NA], in1=t2.ap[:, 0:NA],
                                             op=ALU.add), r=[t1.res, t2.res], w=[t1.res])
        kb.op(DVE, lambda e: e.tensor_tensor(out=v3(t2.ap), in0=v3(hsm.ap), in1=wv(2), op=ALU.mult),
              r=[hsm.res, t2.res], w=[t2.res])
        kb.op(DVE, lambda e: e.tensor_tensor(out=t1.ap[:, 0:NA], in0=t1.ap[:, 0:NA], in1=t2.ap[:, 0:NA],
                                             op=ALU.add), r=[t1.res, t2.res], w=[t1.res])
        kb.op(DVE, lambda e: e.tensor_tensor(out=v3(t1.ap), in0=v3(t1.ap), in1=wv(3), op=ALU.add),
              r=[t1.res], w=[t1.res])
        kb.op(ACT, lambda e: e.activation(out=t2.ap[:, 0:NA], in_=t1.ap[:, 0:NA], func=AF.Gelu),
              r=[t1.res, t2.res], w=[t2.res])
        kb.op(DVE, lambda e: e.tensor_tensor(out=V(gsm.ap, 0, [[1, NA]]), in0=t2.ap[:, 0:NA],
                                             in1=hsm.ap[:, NA:2 * NA], op=ALU.mult),
              r=[t2.res, hsm.res], w=[gsm.res])
        for p in range(6):
            wd = min(512, DFF - p * 512)
            nchk = wd // 128
            bank = ps()

            def fnts(e, bank=bank, nchk=nchk, p=p):
                ins = None
                for ci in range(nchk):
                    c = p * 4 + ci
                    ins = e.transpose(bank.ap[0:NS, ci * 128:(ci + 1) * 128], hsm.ap[:, c * NS:(c + 1) * NS],
                                      ident_f.ap)
                return ins
            kb.op(PE, fnts, r=[hsm.res], w=[bank.res])
            tb = F512.next()
            kb.op(ACT, lambda e: e.activation(out=tb.ap[0:NS, 0:wd], in_=bank.ap[0:NS, 0:wd], func=AF.Copy),
                  r=[bank.res], w=[tb.res])
            kb.dma(SP, cso[l, :, 1, p * 512:p * 512 + wd], tb.ap[0:NS, 0:wd], r=[tb.res])
        for qd in range(4):
            W = ws.next()
            bank = ps()
            mm_group(bank.ap[0:NS, 0:256], [(gsm.ap[:, c, :], W.ap[:, c, :]) for c in range(NCH)],
                     r=[gsm.res, W.res], w=[bank.res])
            resid_add(0, bank, qd * 256, 256, ntok=NS)
            ws.done()
        layer_norm(0, g, NS)

    def gla_sample():
        Wq = ws.next()
        Wk = ws.next()
        g = load_gb(2)
        kb.op(DVE, lambda e: e.tensor_scalar(nbaT.ap, baT.ap, -1.0, None, op0=ALU.mult), w=[nbaT.res])
        bank = ps()
        mm_group(bank.ap[0:16, 0:NS], [(wa1_b.ap[:, kc, :], xT.ap[:, kc, 0:NS]) for kc in range(8)],
                 r=xTr, w=[bank.res])
        kb.op(ACT, lambda e: e.activation(out=rT_b.ap[:, 0:NS], in_=bank.ap[0:16, 0:NS], func=AF.Copy),
              r=[bank.res], w=[rT_b.res])
        bz = ps()

        def fnz(e):
            ins = None
            for h in range(4):
                ins = e.matmul(bz.ap[:, h * NS:(h + 1) * NS], wa2_b.ap[:, h * 128:(h + 1) * 128],
                               rT_b.ap[:, 0:NS], start=True, stop=True)
            return ins
        kb.op(PE, fnz, r=[rT_b.res], w=[bz.res])
        e1 = smallf.ap[:, 3, :]
        eaT = smallf.ap[:, 0, :]
        qts = smallf.ap[:, 1, :]
        kts = smallf.ap[:, 2, :]
        for h in range(4):
            kb.op(ACT, lambda e: e.activation(out=e1[:, h * NS:(h + 1) * NS], in_=bz.ap[:, h * NS:(h + 1) * NS],
                                              func=AF.Exp, scale=-1.0, bias=nbaT.ap[:, h:h + 1]),
                  r=[bz.res, nbaT.res], w=[smallf.res])
        kb.op(ACT, lambda e: e.activation(out=e1, in_=e1, func=AF.Ln, bias=one_c.ap, scale=1.0),
              r=[smallf.res, one_c.res], w=[smallf.res])
        kb.op(ACT, lambda e: e.activation(out=eaT, in_=e1, func=AF.Exp, scale=-1.0 / 16.0),
              r=[smallf.res], w=[smallf.res])
        for (Wx, dst_, sc_) in ((Wq, qts, DKS), (Wk, kts, 1.0)):
            bq = ps()

            def fnq(e, Wx=Wx, bq=bq):
                ins = None
                for h in range(4):
                    for kc in range(8):
                        ins = e.matmul(bq.ap[:, h * NS:(h + 1) * NS], Wx.ap[:, kc, h * 128:(h + 1) * 128],
                                       xT.ap[:, kc, 0:NS], start=(kc == 0), stop=(kc == 7))
                return ins
            kb.op(PE, fnq, r=[xTr[0]] + [Wx.res], w=[bq.res])
            kb.op(DVE, lambda e: e.tensor_scalar(dst_, bq.ap[:, 0:64], sc_, None, op0=ALU.mult),
                  r=[bq.res], w=[smallf.res])
        ws.done()
        ws.done()
        for (dstb, fn_) in ((big4a[0], AF.Copy), (big4b[0], AF.Silu)):
            for vi in range(2):
                W = ws.next()
                bank = ps()
                mm_group(bank.ap[0:NS, :], [(xT.ap[:, kc, 0:NS], W.ap[:, kc, :]) for kc in range(8)],
                         r=[xTr[0]] + [W.res], w=[bank.res])
                kb.op(ACT, lambda e: e.activation(out=dstb.ap[0:NS, vi * 512:(vi + 1) * 512], in_=bank.ap[0:NS, :],
                                                  func=fn_), r=[bank.res], w=[dstb.res])
                ws.done()
        W0 = ws.next()
        W1 = ws.next()
        qm = big4a[1]
        kb.op(DVE, lambda e: e.tensor_tensor(out=V(qm.ap, 0, [[256, 4], [16, 16], [1, 16]]),
                                             in0=V(qts, 0, [[16, 4], [0, 16], [1, 16]]),
                                             in1=V(eye_rep.ap, 0, [[0, 4], [16, 16], [1, 16]]), op=ALU.mult),
              r=[smallf.res], w=[qm.res])
        bo = [ps_reserve(), ps_reserve()]
        S0b = [xtok[1], xtok[2]]
        Snb = [xtok[3], T(V(Sst.ap, 0, [[1, 1024]]), Sst.res)]
        Sbb = [Sbf, T(V(kdec[0].ap, 0, [[1, 512]]), kdec[0].res)]
        Sbb2 = [kdec[1], kdec[2]]
        v4 = lambda ap: V(ap, 0, [[256, 4], [1, 256]])
        for b in range(NS):
            S0 = S0b[b % 2]
            Sn = Snb[b % 2]
            if b == 0 and not s0_prefetched:
                for b2 in range(2):
                    kb.dma(SP, v4(S0b[b2].ap), sg[b2].rearrange("h d v -> d h v"), w=[S0b[b2].res])
            vm = B1024.next()
            kb.op(DVE, lambda e: e.tensor_scalar(vm.ap[0:NS, :], big4a[0].ap[0:NS, :], eye16.ap[:, b:b + 1], None,
                                                 op0=ALU.mult), r=[big4a[0].res], w=[vm.res])
            Sb_lo = Sbb[b % 2] if False else None
            sbf_t = B1024.next()
            for h in range(4):
                bv = ps()
                mm_group(bv.ap[:, 0:256], [(ones128.ap[0:NS, :], vm.ap[0:NS, h * 256:(h + 1) * 256])],
                         r=[ones128.res, vm.res], w=[bv.res])
                tmp = F512.next()
                col = h * NS + b
                kb.op(DVE, lambda e: e.tensor_scalar(tmp.ap[:, 0:256], S0.ap[:, h * 256:(h + 1) * 256],
                                                     eaT[:, col:col + 1], None, op0=ALU.mult),
                      r=[S0.res, smallf.res], w=[tmp.res])
                kb.op(DVE, lambda e: e.scalar_tensor_tensor(out=Sn.ap[:, h * 256:(h + 1) * 256],
                                                            in0=bv.ap[:, 0:256], scalar=kts[:, col:col + 1],
                                                            in1=tmp.ap[:, 0:256], op0=ALU.mult, op1=ALU.add),
                      r=[bv.res, tmp.res, smallf.res], w=[Sn.res])
            if b + 2 < NS:
                kb.dma(SP, v4(S0.ap), sg[b + 2].rearrange("h d v -> d h v"), w=[S0.res])
            kb.op(ACT, lambda e: e.activation(out=sbf_t.ap, in_=V(Sn.ap, 0, [[1, 1024]]), func=AF.Copy),
                  r=[Sn.res], w=[sbf_t.res])
            kb.dma(SP, gs[b].rearrange("h d v -> d h v"), v4(Sn.ap), r=[Sn.res])

            def fno(e, b=b, sbf_t=sbf_t):
                ins = None
                for h in range(4):
                    ins = e.matmul(bo[h // 2].ap[0:NS, (h % 2) * 256:(h % 2 + 1) * 256],
                                   V(qm.ap, h * 256 + b * 16, [[1, 16]]), sbf_t.ap[:, h * 256:(h + 1) * 256],
                                   start=(b == 0 and h % 2 == 0), stop=(b == NS - 1), skip_group_check=True)
                return ins
            kb.op(PE, fno, r=[qm.res, sbf_t.res], w=[bo[0].res, bo[1].res])
        st = stat[0]
        kb.op(DVE, lambda e: e.memset(st.ap[:, 16:20], 0.0), w=[st.res])
        for h in range(4):
            junk = F512.next()
            kb.op(ACT, lambda e: e.activation(out=junk.ap[0:NS, 0:256],
                                              in_=bo[h // 2].ap[0:NS, (h % 2) * 256:(h % 2 + 1) * 256],
                                              func=AF.Square, accum_out=st.ap[0:NS, 16 + h:17 + h]),
                  r=[bo[h // 2].res], w=[junk.res, st.res])
        kb.op(ACT, lambda e: e.activation(out=st.ap[0:NS, 20:24], in_=st.ap[0:NS, 16:20], func=AF.Ln,
                                          bias=eps_nm.ap[0:NS, :], scale=1.0 / 256.0), r=[st.res, eps_nm.res],
              w=[st.res])
        kb.op(ACT, lambda e: e.activation(out=st.ap[0:NS, 20:24], in_=st.ap[0:NS, 20:24], func=AF.Exp,
                                          scale=-0.5), r=[st.res], w=[st.res])
        ogb = B1024.next()
        for hp in range(2):
            ogt = F512.next()
            for h in (2 * hp, 2 * hp + 1):
                kb.op(DVE, lambda e: e.scalar_tensor_tensor(
                    out=ogt.ap[0:NS, (h % 2) * 256:(h % 2 + 1) * 256],
                    in0=bo[hp].ap[0:NS, (h % 2) * 256:(h % 2 + 1) * 256], scalar=st.ap[0:NS, 20 + h:21 + h],
                    in1=ng_rep.ap[0:NS, :], op0=ALU.mult, op1=ALU.mult), r=[bo[hp].res, st.res], w=[ogt.res])
            kb.op(DVE, lambda e: e.tensor_tensor(out=ogb.ap[0:NS, hp * 512:(hp + 1) * 512], in0=ogt.ap[0:NS, :],
                                                 in1=big4b[0].ap[0:NS, hp * 512:(hp + 1) * 512], op=ALU.mult),
                  r=[ogt.res, big4b[0].res], w=[ogb.res])
        ps_release(bo[0])
        ps_release(bo[1])
        bank = ps()
        bb = bank.ap.bitcast(BF16)

        def fnt(e):
            ins = None
            for c in range(8):
                ins = e.transpose(bb[:, c * NS:(c + 1) * NS], ogb.ap[0:NS, c * 128:(c + 1) * 128],
                                  ident.ap[0:NS, 0:NS])
            return ins
        kb.op(PE, fnt, r=[ogb.res], w=[bank.res])
        kb.op(ACT, lambda e: e.activation(out=V(ogTs.ap, 0, [[1, 128]]), in_=bb[:, 0:128], func=AF.Copy),
              r=[bank.res], w=[ogTs.res])
        for half, W in enumerate((W0, W1)):
            bank = ps()
            mm_group(bank.ap[0:NS, :], [(ogTs.ap[:, c, :], W.ap[:, c, :]) for c in range(8)],
                     r=[ogTs.res, W.res], w=[bank.res])
            resid_add(0, bank, half * 512, 512, ntok=NS)
        layer_norm(0, g, NS)
        make_xT(0, NS)
        ws.done()
        ws.done()

    s0_prefetched = []

    def prefetch_s0():
        for b in range(2):
            kb.dma(SP, V(xtok[1 + b].ap, 0, [[256, 4], [1, 256]]), sg[b].rearrange("h d v -> d h v"),
                   w=[xtok[1 + b].res])
        s0_prefetched.append(1)

    def sample_tile():
        kb.dma(SP, xtok[0].ap[0:NS, :], xs, w=[xtok[0].res])
        make_xT(0, NS)
        attn_sample()
        if nlayers > 1 and SLV >= 7:
            prefetch_s0()
        if SLV >= 6:
            ffn_sample(0, 1)
        if nlayers > 1 and SLV >= 7:
            make_xT(0, NS)
            gla_sample()
            if SLV >= 8:
                ffn_sample(1, 3)
        kb.dma(SP, ys, xtok[0].ap[0:NS, :], r=[xtok[0].res])

    def prefetch_x(s, t):
        for j in range(4):
            n = 4 * t + j
            kb.dma(POOL, big4b[j].ap, xp[s, n * 128:(n + 1) * 128, :], w=[big4b[j].res])

    tiles = [(s, t) for s in range(nseq) for t in range(ntiles_per_seq)]
    for ti, (s, t) in enumerate(tiles):
        if True:
            for j in range(4):
                n = 4 * t + j
                kb.dma(SP, xtok[j].ap, xp[s, n * 128:(n + 1) * 128, :], w=[xtok[j].res])
            for j in range(4):
                make_xT(j, src_bf=big4b[j])
            attn_prompt(s, t)
            if nlayers == 1 and ti + 1 < len(tiles):
                prefetch_x(*tiles[ti + 1])
            def store_y(j, s=s, t=t):
                n = 4 * t + j
                kb.dma(SP, yp[s, n * 128:(n + 1) * 128, :], xtok[j].ap, r=[xtok[j].res])

            if nlayers > 1:
                ffn_prompt(0, s, t, 1, after=make_xT)
                gla_prompt(s, t)
                nxt = (lambda ti=ti: prefetch_x(*tiles[ti + 1])) if ti + 1 < len(tiles) else None
                ffn_prompt(1, s, t, 3, after=store_y, mid=nxt)
            else:
                ffn_prompt(0, s, t, 1, after=store_y)

    if do_sample:
        sample_tile()

    for sl in kb.slots:
        if sl.cnt > 0:
            SP.eng.wait_ge(sl.sem, sl.cnt)
    for q in (PE, ACT, DVE, POOL):
        if q.cnt > 0:
            SP.eng.wait_ge(q.sem, q.cnt)
    return kb


def _consts():
    c = {}
    c["c_ident"] = np.eye(128, dtype=np.float32)
    inv = (1.0 / (10000.0 ** (np.arange(0, 64, 2, dtype=np.float32) / np.float32(64)))).astype(np.float32)

    def tables(pos):
        ang = pos.astype(np.float32)[:, None] * inv[None, :]
        cs = np.cos(ang).astype(np.float32)
        sn = np.sin(ang).astype(np.float32)
        return np.concatenate([cs, cs], -1), np.concatenate([-sn, sn], -1)
    pc, ps_ = tables(np.arange(SEQ, dtype=np.float32))
    c["c_ropec"] = np.ascontiguousarray(pc.reshape(16, 128, 64).transpose(1, 0, 2))
    c["c_ropes"] = np.ascontiguousarray(ps_.reshape(16, 128, 64).transpose(1, 0, 2))
    sc_, ss_ = tables(np.full((NSMP,), 16384.0, dtype=np.float32))
    c["c_ropecs"] = sc_
    c["c_ropess"] = ss_
    sidx = np.arange(128)[:, None]
    qidx = np.arange(128)[None, :]
    mp = np.where(sidx > qidx, 1.0, 0.0).astype(np.float32)
    mc = np.where(sidx <= qidx, 1.0, 0.0).astype(np.float32)
    c["c_maskp"] = np.tile(mp, (1, 4))
    c["c_maskc"] = np.tile(mc, (1, 4))
    c["c_trim"] = np.tile((sidx <= qidx).astype(np.float32), (1, 4))
    c["c_utri"] = np.where(sidx <= qidx, -1.0 / 16.0, 0.0).astype(np.float32)
    c["c_lstr"] = np.where(sidx > qidx, -1.0 / 16.0, 0.0).astype(np.float32)
    c["c_eye16"] = np.eye(NSMP, dtype=np.float32)
    return c


_CACHE = {}


def kernel(x_prompt, x_sample, cache_k, cache_v, state_gla, state_conv,
           attn_w_qkv, attn_b_qkv, attn_sinks, attn_w_o,
           gla_w_in, gla_w_a1, gla_w_a2, gla_b_a, gla_norm_g, gla_w_o,
           ffn_w_up, ffn_conv_w, ffn_conv_b, ffn_w_down,
           ln_mix_g, ln_mix_b, ln_ffn_g, ln_ffn_b):
    f = lambda a: np.ascontiguousarray(np.asarray(a, dtype=np.float32))
    nt = int(os.environ.get("MK_NT", "4"))
    nsq = int(os.environ.get("MK_NSEQ", "2"))
    nl = int(os.environ.get("MK_NL", "2"))
    smp = int(os.environ.get("MK_SMP", "1")) == 1
    key = (nt, nsq, nl, smp)
    if key not in _CACHE:
        _CACHE[key] = build(nt, nsq, smp, nl)
    kb = _CACHE[key]
    cwt = np.concatenate([f(ffn_conv_w), f(ffn_conv_b)[:, None, :]], axis=1)
    cwt = np.ascontiguousarray(cwt.reshape(2, 4, NCH, 128).transpose(0, 3, 2, 1))
    lng = np.ascontiguousarray(np.stack([f(ln_mix_g)[0], f(ln_ffn_g)[0], f(ln_mix_g)[1], f(ln_ffn_g)[1]]))
    lnb = np.ascontiguousarray(np.stack([f(ln_mix_b)[0], f(ln_ffn_b)[0], f(ln_mix_b)[1], f(ln_ffn_b)[1]]))
    shared = dict(
        wqkv=f(attn_w_qkv)[0], bqkv=f(attn_b_qkv), sinks=f(attn_sinks), woa=f(attn_w_o)[0],
        win=f(gla_w_in)[0], wa1=f(gla_w_a1)[0], wa2=f(gla_w_a2)[0], ba=f(gla_b_a), ng=f(gla_norm_g),
        wog=f(gla_w_o)[0], wup=f(ffn_w_up), cwt=cwt, wdn=f(ffn_w_down), lng=lng, lnb=lnb)
    shared.update(_consts())
    xp = f(x_prompt)
    xs = f(x_sample)
    ck = f(cache_k)
    cv = f(cache_v)
    sg = f(state_gla)
    sc = f(state_conv)
    in_maps = []
    for c in range(NCORES):
        m = dict(shared)
        m["xp"] = xp[NSEQ * c:NSEQ * (c + 1)]
        m["xs"] = xs[NSMP * c:NSMP * (c + 1), 0, :]
        m["ck"] = ck[0, NSMP * c:NSMP * (c + 1)].reshape(NSMP, 128, 128)
        m["cv"] = cv[0, NSMP * c:NSMP * (c + 1)].reshape(NSMP, 128, 128)
        m["sg"] = sg[0, NSMP * c:NSMP * (c + 1)]
        m["sc"] = np.ascontiguousarray(sc[:, NSMP * c:NSMP * (c + 1)])
        in_maps.append({k: np.ascontiguousarray(v) for k, v in m.items()})
    res = run_bass_kernel_spmd(kb.nc, in_maps, core_ids=list(range(NCORES)))
    R = res.results
    cat = lambda k, ax=0: np.concatenate([r[k] for r in R], axis=ax)
    y_prompt = cat("yp")
    y_sample = cat("ys").reshape(128, 1, D)
    nkp = cat("nkp").reshape(1, 16, 128, 2, 64)
    nvp = cat("nvp").reshape(1, 16, 128, 2, 64)
    nks = cat("nks").reshape(1, 128, 128, 2, 64)
    nvs = cat("nvs").reshape(1, 128, 128, 2, 64)
    gpo = cat("gp").reshape(1, 16, 4, 128, 256)
    gso = cat("gs").reshape(1, 128, 4, 128, 256)
    cpo = cat("cpo", 1)
    cso = cat("cso", 1)
    return (y_prompt, y_sample, nkp, nvp, nks, nvs, gpo, gso, cpo, cso)
```

```python
import os
import numpy as np
from contextlib import ExitStack
import concourse.bass as bass
import concourse.mybir as mybir
from concourse.bass_utils import run_bass_kernel_spmd

F32 = mybir.dt.float32
BF16 = mybir.dt.bfloat16
AF = mybir.ActivationFunctionType
ALU = mybir.AluOpType

NCORES = 8
D = 1024
SEQ = 2048
NSEQ = 2
NSMP = 16
DFF = 2816
NCH = 22
ALPHA = 4.0 ** 0.25
LN_EPS = 1e-5
NORM_EPS = 1e-6
ATT_SCALE = 0.125
DKS = 128.0 ** -0.5
MASKV = -2000.0
RING = 4
SLOTW = NCH * 256


class Res:
    __slots__ = ("name", "w", "rs", "excl", "slot")

    def __init__(self, name, excl=False):
        self.name = name
        self.w = None
        self.rs = {}
        self.excl = excl
        self.slot = None


class Prod:
    def __init__(self, sem):
        self.sem = sem
        self.cnt = 0


class Q(Prod):
    def __init__(self, eng, sem):
        super().__init__(sem)
        self.eng = eng
        self.seen = {}


class T:
    def __init__(self, ap, res):
        self.ap = ap
        self.res = res

    def __getitem__(self, k):
        return self.ap[k]


def V(ap, off, dims, npart=None, pstart=0):
    p = ap.ap[0]
    n = npart if npart is not None else p[1]
    return bass.AP(ap.tensor, ap.offset + pstart * p[0] + off, [[p[0], n]] + [list(d) for d in dims])


class KB:
    def __init__(self):
        self.nc = bass.Bass("TRN2", target_bir_lowering=False)
        self.es = ExitStack()
        nc = self.nc
        self.PE = Q(nc.tensor, self.sem("pe"))
        self.ACT = Q(nc.scalar, self.sem("act"))
        self.DVE = Q(nc.vector, self.sem("dve"))
        self.POOL = Q(nc.gpsimd, self.sem("pool"))
        self.SP = Q(nc.sync, self.sem("sp"))
        self.slots = []
        self.cslot = self.newslot("const")
        self.misc = self.newslot("misc")
        self.nt = 0

    def sem(self, name):
        return self.es.enter_context(self.nc.semaphore(name))

    def newslot(self, name):
        s = Prod(self.sem("d_" + name))
        self.slots.append(s)
        return s

    def sb(self, shape, dt, name=None, track=True):
        self.nt += 1
        nm = name or ("t%d" % self.nt)
        t = self.es.enter_context(self.nc.sbuf_tensor(nm, list(shape), dt))
        return T(t[:], Res(nm) if track else None)

    def _wait(self, q, r, w):
        d = {}

        def add(pr, c):
            if d.get(pr, 0) < c:
                d[pr] = c

        for x in r:
            if x is None:
                continue
            if x.w is not None:
                add(*x.w)
            if x.excl:
                for pr, c in x.rs.items():
                    add(pr, c)
        for x in w:
            if x is None:
                continue
            if x.w is not None:
                add(*x.w)
            for pr, c in x.rs.items():
                add(pr, c)
        for pr, c in d.items():
            if q.seen.get(pr, 0) >= c:
                continue
            q.eng.wait_ge(pr.sem, c)
            q.seen[pr] = c

    def _mark(self, prod, c, r, w):
        for x in r:
            if x is None:
                continue
            if x.excl:
                x.w = (prod, c)
                x.rs = {}
            else:
                x.rs[prod] = c
        for x in w:
            if x is None:
                continue
            x.w = (prod, c)
            x.rs = {}

    def op(self, q, fn, r=(), w=()):
        self._wait(q, r, w)
        ins = fn(q.eng)
        q.cnt += 1
        ins.then_inc(q.sem, 1)
        self._mark(q, q.cnt, r, w)

    def dma(self, q, out, in_, r=(), w=(), slot=None, **kw):
        if slot is None:
            for x in list(w) + list(r):
                if x is not None:
                    if x.slot is None:
                        x.slot = self.newslot(x.name)
                    slot = x.slot
                    break
        if slot is None:
            slot = self.misc
        self._wait(q, r, w)
        ins = q.eng.dma_start(out=out, in_=in_, **kw)
        slot.cnt += 16
        ins.then_inc(slot.sem, 16)
        self._mark(slot, slot.cnt, r, w)


def build(ntiles_per_seq=4, nseq=NSEQ, do_sample=True, nlayers=2):
    kb = KB()
    nc = kb.nc
    PE, ACT, DVE, POOL, SP = kb.PE, kb.ACT, kb.DVE, kb.POOL, kb.SP

    def din(name, shape):
        return nc.dram_tensor(name, list(shape), F32, kind="ExternalInput").ap()

    def dout(name, shape):
        return nc.dram_tensor(name, list(shape), F32, kind="ExternalOutput").ap()

    xp = din("xp", [NSEQ, SEQ, D])
    xs = din("xs", [NSMP, D])
    ck = din("ck", [NSMP, 128, 128])
    cv = din("cv", [NSMP, 128, 128])
    sg = din("sg", [NSMP, 4, 128, 256])
    sc = din("sc", [2, NSMP, 2, DFF])
    wqkv = din("wqkv", [D, 1280])
    bqkv = din("bqkv", [1, 1280])
    sinks = din("sinks", [1, 16])
    woa = din("woa", [D, D])
    win = din("win", [D, 3072])
    wa1 = din("wa1", [D, 16])
    wa2 = din("wa2", [16, 512])
    ba = din("ba", [1, 512])
    ng = din("ng", [1, 256])
    wog = din("wog", [D, D])
    wup = din("wup", [2, D, 2 * DFF])
    cwt = din("cwt", [2, 128, NCH, 4])
    wdn = din("wdn", [2, DFF, D])
    lng = din("lng", [4, D])
    lnb = din("lnb", [4, D])
    c_ident = din("c_ident", [128, 128])
    c_ropec = din("c_ropec", [128, 16, 64])
    c_ropes = din("c_ropes", [128, 16, 64])
    c_ropecs = din("c_ropecs", [NSMP, 64])
    c_ropess = din("c_ropess", [NSMP, 64])
    c_maskp = din("c_maskp", [128, 512])
    c_maskc = din("c_maskc", [128, 512])
    c_trim = din("c_trim", [128, 512])
    c_utri = din("c_utri", [128, 128])
    c_lstr = din("c_lstr", [128, 128])
    c_eye16 = din("c_eye16", [NSMP, NSMP])

    yp = dout("yp", [NSEQ, SEQ, D])
    ys = dout("ys", [NSMP, D])
    nkp = dout("nkp", [NSEQ, 128, 128])
    nvp = dout("nvp", [NSEQ, 128, 128])
    nks = dout("nks", [NSMP, 128, 128])
    nvs = dout("nvs", [NSMP, 128, 128])
    gp = dout("gp", [NSEQ, 4, 128, 256])
    gs = dout("gs", [NSMP, 4, 128, 256])
    cpo = dout("cpo", [2, NSEQ, 2, DFF])
    cso = dout("cso", [2, NSMP, 2, DFF])

    cslot2 = kb.newslot("const_sw")

    def cload(q, t, src):
        ins = q.eng.dma_start(out=t.ap, in_=src)
        sl = cslot2 if q is POOL else kb.cslot
        sl.cnt += 16
        ins.then_inc(sl.sem, 16)

    ident = kb.sb([128, 128], BF16, "ident", False)
    maskp = kb.sb([128, 512], BF16, "maskp", False)
    maskc = kb.sb([128, 512], BF16, "maskc", False)
    trim = kb.sb([128, 512], BF16, "trim", False)
    utri = kb.sb([128, 128], F32, "utri", False)
    lstr = kb.sb([128, 128], F32, "lstr", False)
    ropec = kb.sb([128, 16, 64], F32, "ropec", False)
    ropes = kb.sb([128, 16, 64], F32, "ropes", False)
    ropecs = kb.sb([NSMP, 64], F32, "ropecs", False)
    ropess = kb.sb([NSMP, 64], F32, "ropess", False)
    bq_b = kb.sb([1, 1280], BF16, "bq_b", False)
    ba_b = kb.sb([1, 512], BF16, "ba_b", False)
    wa1_b = kb.sb([128, 8, 16], BF16, "wa1_b", False)
    wa2_b = kb.sb([16, 512], BF16, "wa2_b", False)
    ng_rep = kb.sb([128, 256], F32, "ng_rep", False)
    sink_rep = kb.sb([128, 16], F32, "sink_rep", False)
    cw = [kb.sb([128, NCH, 4], F32, "cw%d" % l, False) for l in range(2)]
    eye16 = kb.sb([NSMP, NSMP], F32, "eye16", False)
    esink = kb.sb([128, 16], F32, "esink")
    ones1 = kb.sb([1, 128], BF16, "ones1")
    ones128 = kb.sb([128, 128], BF16, "ones128")

    cload(POOL, ident, c_ident)
    cload(POOL, maskp, c_maskp)
    cload(POOL, maskc, c_maskc)
    cload(POOL, trim, c_trim)
    cload(SP, utri, c_utri)
    cload(SP, lstr, c_lstr)
    cload(SP, ropec, c_ropec)
    cload(SP, ropes, c_ropes)
    cload(SP, ropecs, c_ropecs)
    cload(SP, ropess, c_ropess)
    cload(POOL, bq_b, bqkv)
    cload(POOL, ba_b, ba)
    cload(POOL, wa1_b, wa1.rearrange("(kc p) r -> p kc r", p=128))
    cload(POOL, wa2_b, wa2)
    cload(SP, ng_rep, ng[0:1, :].partition_broadcast(128))
    cload(SP, sink_rep, sinks[0:1, :].partition_broadcast(128))
    cload(SP, cw[0], cwt[0])
    cload(SP, cw[1], cwt[1])
    cload(SP, eye16, c_eye16)
    ident_f = kb.sb([128, 128], F32, "ident_f", False)
    cload(SP, ident_f, c_ident)
    eye_rep = kb.sb([128, 256], F32, "eye_rep", False)
    baT = kb.sb([128, 4], F32, "baT", False)
    cload(SP, eye_rep, c_eye16.rearrange("a b -> (a b)").rearrange("(o n) -> o n", o=1).partition_broadcast(128))
    with nc.allow_non_contiguous_dma(reason="tiny transposed bias load"):
        cload(SP, baT, ba[0, :].rearrange("(h d) -> d h", d=128))
    for q in (PE, ACT, DVE, POOL, SP):
        q.eng.wait_ge(kb.cslot.sem, kb.cslot.cnt)
        q.eng.wait_ge(cslot2.sem, cslot2.cnt)

    kb.op(ACT, lambda e: e.activation(out=esink.ap, in_=sink_rep.ap, func=AF.Exp), w=[esink.res])
    kb.op(DVE, lambda e: e.memset(ones1.ap, 1.0), w=[ones1.res])
    kb.op(DVE, lambda e: e.memset(ones128.ap, 1.0), w=[ones128.res])

    xtok = [kb.sb([128, D], F32, "xtok%d" % j) for j in range(4)]
    xT = kb.sb([128, 8, 512], BF16, "xT")
    xTr = [Res("xT%d" % j) for j in range(4)]
    gT = kb.sb([128, NCH, 512], BF16, "gT")
    gTres = [Res("gT%d" % c) for c in range(NCH)]
    ring = [kb.sb([128, SLOTW], BF16, "ring%d" % i) for i in range(RING)]
    gbuf = [kb.sb([128, 2, D], F32, "gb%d" % i) for i in range(2)]
    big4a = [kb.sb([128, D], BF16, "big4a%d" % j) for j in range(4)]
    big4b = [kb.sb([128, D], BF16, "big4b%d" % j) for j in range(4)]
    k_r = kb.sb([128, 4, 128], BF16, "k_r")
    KT = kb.sb([128, 8, 128], BF16, "KT")
    KTres = [Res("KT%d" % i) for i in range(8)]
    Vd = kb.sb([128, 8, 2, 128], BF16, "Vd")
    Vdres = [Res("Vd%d" % i) for i in range(8)]
    kf = kb.sb([128, 128], F32, "kf")
    vf = kb.sb([128, 128], F32, "vf")
    qinT = kb.sb([128, 4, 512], BF16, "qinT")
    kinT = kb.sb([128, 4, 512], BF16, "kinT")
    kdec = [kb.sb([128, 512], BF16, "kdec%d" % j) for j in range(4)]
    ebl = kb.sb([128, 4, 4], F32, "ebl")
    Sst = kb.sb([128, 4, 256], F32, "Sst")
    Sbf = kb.sb([128, 4, 256], BF16, "Sbf")
    Sbf2 = [Sbf, kb.sb([128, 4, 256], BF16, "Sbf_b")]
    rT_b = kb.sb([16, 512], BF16, "rT_b")
    aprev = [kb.sb([128, NCH, 2], F32, "aprev%d" % l) for l in range(2)]
    abuf = [kb.sb([128, 514], F32, "abuf%d" % i) for i in range(3)]
    abufh = [Res("abufh%d" % i) for i in range(3)]
    stat = [kb.sb([128, 24], F32, "stat%d" % i) for i in range(2)]

    ksm = kb.sb([NSMP, 128], F32, "ksm")
    vsm = kb.sb([NSMP, 128], F32, "vsm")
    ots = kb.sb([128, 8, NSMP], BF16, "ots")
    gsm = kb.sb([128, NCH, NSMP], BF16, "gsm")
    nbaT = kb.sb([128, 4], F32, "nbaT")
    smallf = kb.sb([128, 4, 64], F32, "smallf")
    ogTs = kb.sb([128, 8, NSMP], BF16, "ogTs")
    hsm = T(V(qinT.ap, 0, [[1, 2048]]).bitcast(F32), qinT.res)
    cst = T(V(kinT.ap, 0, [[1, 2048]]).bitcast(F32), kinT.res)

    class Pool:
        def __init__(self, n, shape, dt, name):
            self.b = [kb.sb(shape, dt, "%s%d" % (name, i)) for i in range(n)]
            self.i = 0

        def next(self):
            t = self.b[self.i % len(self.b)]
            self.i += 1
            return t

    F512 = Pool(6, [128, 512], F32, "f512_")
    rdp = Pool(4, [128, 4], F32, "rd4_")
    kb.op(DVE, lambda e: e.memset(V(Vd.ap, 0, [[1, 2048]]), 1.0), w=Vdres)
    B512 = Pool(5, [128, 512], BF16, "b512_")
    B1024 = Pool(4, [128, 1024], BF16, "b1024_")

    psum = []
    for i in range(8):
        t = kb.es.enter_context(nc.psum_tensor("ps%d" % i, [128, 512], F32))
        psum.append(T(t[:], Res("ps%d" % i, excl=True)))
    psi = [0]

    reserved = set()

    def ps():
        while True:
            i = psi[0] % 8
            psi[0] += 1
            if i not in reserved:
                return psum[i]

    def ps_reserve():
        t = ps()
        reserved.add(psum.index(t))
        return t

    def ps_release(t):
        reserved.discard(psum.index(t))

    def slab_list():
        L = []
        for i, (c0, wd) in enumerate([(0, 512), (512, 512), (1024, 256)]):
            L.append((wqkv[:, c0:c0 + wd], 8, wd))
        for h in range(2):
            L.append((woa[:, h * 512:(h + 1) * 512], 8, 512))
        for s in range(11):
            L.append((wup[0, :, s * 512:(s + 1) * 512], 8, 512))
        for qd in range(4):
            L.append((wdn[0, :, qd * 256:(qd + 1) * 256], NCH, 256))
        if nlayers > 1:
            for i in range(6):
                L.append((win[:, i * 512:(i + 1) * 512], 8, 512))
            for h in range(2):
                L.append((wog[:, h * 512:(h + 1) * 512], 8, 512))
            for s in range(11):
                L.append((wup[1, :, s * 512:(s + 1) * 512], 8, 512))
            for qd in range(4):
                L.append((wdn[1, :, qd * 256:(qd + 1) * 256], NCH, 256))
        return L

    n_passes = nseq * ntiles_per_seq + (1 if do_sample else 0)
    stream = slab_list() * n_passes

    class WS:
        def __init__(self):
            self.pos = 0
            self.rel = 0
            self.issued = 0
            for _ in range(RING):
                self.issue()

        def issue(self):
            if self.issued >= len(stream):
                return
            src, kc, wd = stream[self.issued]
            slot = ring[self.issued % RING]
            dst = V(slot.ap, 0, [[wd, kc], [1, wd]])
            kb.dma(POOL, dst, src.rearrange("(kc p) c -> p kc c", p=128), w=[slot.res])
            self.issued += 1

        def next(self):
            src, kc, wd = stream[self.pos]
            slot = ring[self.pos % RING]
            self.pos += 1
            return T(V(slot.ap, 0, [[wd, kc], [1, wd]]), slot.res)

        def done(self):
            self.rel += 1
            self.issue()

    if nseq * ntiles_per_seq > 0:
        for j in range(4):
            kb.dma(POOL, big4b[j].ap, xp[0, j * 128:(j + 1) * 128, :], w=[big4b[j].res])
    ws = WS()

    def mm_group(out_ap, pairs, r, w, first=True, last=True):
        n = len(pairs)

        def fn(e):
            ins = None
            for i, (l, rr) in enumerate(pairs):
                ins = e.matmul(out_ap, l, rr, start=(first and i == 0), stop=(last and i == n - 1))
            return ins
        kb.op(PE, fn, r=r, w=w)

    def make_xT(j, ntok=128, src_bf=None):
        if src_bf is not None:
            xb = src_bf
        else:
            xb = B1024.next()
            kb.op(ACT, lambda e: e.activation(out=xb.ap[0:ntok, :], in_=xtok[j].ap[0:ntok, :], func=AF.Copy),
                  r=[xtok[j].res], w=[xb.res])
        bank = ps()
        bb = bank.ap.bitcast(BF16)

        def fn(e):
            ins = None
            for kc in range(8):
                ins = e.transpose(bb[:, kc * 128:kc * 128 + ntok], xb.ap[0:ntok, kc * 128:(kc + 1) * 128],
                                  ident.ap[0:ntok, 0:ntok])
            return ins
        kb.op(PE, fn, r=[xb.res], w=[bank.res])
        kb.op(DVE, lambda e: e.tensor_copy(V(xT.ap, j * 128, [[512, 8], [1, ntok]]),
                                           V(bb, 0, [[128, 8], [1, ntok]])),
              r=[bank.res], w=[xTr[j]])

    gb_state = {"i": 0}

    def load_gb(idx):
        g = gbuf[gb_state["i"] % 2]
        gb_state["i"] += 1
        kb.dma(SP, g.ap[:, 0, :], lng[idx:idx + 1, :].partition_broadcast(128), w=[g.res])
        kb.dma(SP, g.ap[:, 1, :], lnb[idx:idx + 1, :].partition_broadcast(128), w=[g.res])
        return g

    def resid_add(j, bank, c0, wd, ntok=128):
        kb.op(DVE, lambda e: e.scalar_tensor_tensor(out=xtok[j].ap[0:ntok, c0:c0 + wd],
                                                    in0=xtok[j].ap[0:ntok, c0:c0 + wd], scalar=ALPHA,
                                                    in1=bank.ap[0:ntok, 0:wd], op0=ALU.mult, op1=ALU.add),
              r=[bank.res, xtok[j].res], w=[xtok[j].res])

    def layer_norm(j, g, ntok=128):
        st = stat[j % 2]
        x = xtok[j]
        for hh in range(2):
            kb.op(DVE, lambda e: e.bn_stats(st.ap[0:ntok, hh * 6:(hh + 1) * 6],
                                            x.ap[0:ntok, hh * 512:(hh + 1) * 512]),
                  r=[x.res], w=[st.res])
        kb.op(DVE, lambda e: e.bn_aggr(st.ap[0:ntok, 12:14], st.ap[0:ntok, 0:12]), r=[st.res], w=[st.res])
        kb.op(ACT, lambda e: e.activation(out=st.ap[0:ntok, 14:15], in_=st.ap[0:ntok, 13:14], func=AF.Ln,
                                          bias=eps_ln.ap[0:ntok, :], scale=1.0), r=[st.res], w=[st.res])
        kb.op(ACT, lambda e: e.activation(out=st.ap[0:ntok, 15:16], in_=st.ap[0:ntok, 14:15], func=AF.Exp,
                                          scale=-0.5), r=[st.res], w=[st.res])
        kb.op(DVE, lambda e: e.scalar_tensor_tensor(out=x.ap[0:ntok, :], in0=x.ap[0:ntok, :],
                                                    scalar=st.ap[0:ntok, 12:13], in1=g.ap[0:ntok, 0, :],
                                                    op0=ALU.subtract, op1=ALU.mult),
              r=[x.res, st.res, g.res], w=[x.res])
        kb.op(DVE, lambda e: e.scalar_tensor_tensor(out=x.ap[0:ntok, :], in0=x.ap[0:ntok, :],
                                                    scalar=st.ap[0:ntok, 15:16], in1=g.ap[0:ntok, 1, :],
                                                    op0=ALU.mult, op1=ALU.add),
              r=[x.res, st.res, g.res], w=[x.res])

    eps_ln = kb.sb([128, 1], F32, "eps_ln")
    eps_nm = kb.sb([128, 1], F32, "eps_nm")
    one_c = kb.sb([128, 1], F32, "one_c")
    kb.op(DVE, lambda e: e.memset(eps_ln.ap, LN_EPS), w=[eps_ln.res])
    kb.op(DVE, lambda e: e.memset(eps_nm.ap, NORM_EPS), w=[eps_nm.res])
    kb.op(DVE, lambda e: e.memset(one_c.ap, 1.0), w=[one_c.res])

    def rope(src, nh, dst, cosap, sinap, ntok=128):
        A = F512.next()
        B = F512.next()
        srcb, res_src = src
        cosb = V(cosap, 0, [[0, nh], [1, 64]], npart=ntok)
        sinb = V(sinap, 0, [[0, nh], [1, 64]], npart=ntok)
        kb.op(DVE, lambda e: e.tensor_tensor(out=V(A.ap, 0, [[64, nh], [1, 64]], npart=ntok),
                                             in0=V(srcb, 0, [[64, nh], [1, 64]], npart=ntok), in1=cosb,
                                             op=ALU.mult), r=[res_src], w=[A.res])
        kb.op(DVE, lambda e: e.tensor_tensor(out=V(B.ap, 0, [[64, nh], [32, 2], [1, 32]], npart=ntok),
                                             in0=V(srcb, 32, [[64, nh], [-32, 2], [1, 32]], npart=ntok),
                                             in1=V(sinap, 0, [[0, nh], [32, 2], [1, 32]], npart=ntok),
                                             op=ALU.mult), r=[res_src], w=[B.res])
        dap, dres = dst
        kb.op(DVE, lambda e: e.tensor_tensor(out=dap, in0=V(A.ap, 0, [[64, nh], [1, 64]], npart=ntok),
                                             in1=V(B.ap, 0, [[64, nh], [1, 64]], npart=ntok), op=ALU.add),
              r=[A.res, B.res], w=dres)

    def attn_prompt(s, t):
        for si in range(2):
            W = ws.next()
            for j in range(4):
                n = 4 * t + j
                bank = ps()
                pairs = [(xT.ap[:, kc, j * 128:(j + 1) * 128], W.ap[:, kc, :]) for kc in range(8)]
                pairs.append((ones1.ap[0:1, :], bq_b.ap[0:1, si * 512:(si + 1) * 512]))
                mm_group(bank.ap, pairs, r=[xTr[j]] + [W.res, ones1.res], w=[bank.res])
                dst = V(big4a[j].ap, si * 64, [[128, 8], [1, 64]])
                rope((bank.ap, bank.res), 8, (dst, [big4a[j].res]), ropec.ap[:, n, :], ropes.ap[:, n, :])
            ws.done()
        W = ws.next()
        for j in range(4):
            n = 4 * t + j
            slot = n % 8
            bank = ps()
            pairs = [(xT.ap[:, kc, j * 128:(j + 1) * 128], W.ap[:, kc, :]) for kc in range(8)]
            pairs.append((ones1.ap[0:1, :], bq_b.ap[0:1, 1024:1280]))
            mm_group(bank.ap[:, 0:256], pairs, r=[xTr[j]] + [W.res, ones1.res], w=[bank.res])
            rope((bank.ap[:, 0:128], bank.res), 2, (V(k_r.ap, j * 128, [[64, 2], [1, 64]]), [k_r.res]),
                 ropec.ap[:, n, :], ropes.ap[:, n, :])
            kb.op(ACT, lambda e: e.activation(out=V(Vd.ap, slot * 256, [[128, 2], [1, 64]]),
                                              in_=V(bank.ap, 128, [[64, 2], [1, 64]]), func=AF.Copy),
                  r=[bank.res], w=[Vdres[slot]])
            if n == 15:
                A = F512.next()
                B = F512.next()
                kb.op(DVE, lambda e: e.tensor_tensor(out=V(A.ap, 0, [[64, 2], [1, 64]]),
                                                     in0=V(bank.ap, 0, [[64, 2], [1, 64]]),
                                                     in1=V(ropec.ap[:, n, :], 0, [[0, 2], [1, 64]]), op=ALU.mult),
                      r=[bank.res], w=[A.res])
                kb.op(DVE, lambda e: e.tensor_tensor(out=V(B.ap, 0, [[64, 2], [32, 2], [1, 32]]),
                                                     in0=V(bank.ap, 32, [[64, 2], [-32, 2], [1, 32]]),
                                                     in1=V(ropes.ap[:, n, :], 0, [[0, 2], [32, 2], [1, 32]]),
                                                     op=ALU.mult), r=[bank.res], w=[B.res])
                kb.op(DVE, lambda e: e.tensor_tensor(out=kf.ap, in0=A.ap[:, 0:128], in1=B.ap[:, 0:128],
                                                     op=ALU.add), r=[A.res, B.res], w=[kf.res])
                kb.op(ACT, lambda e: e.activation(out=vf.ap, in_=bank.ap[:, 128:256], func=AF.Copy),
                      r=[bank.res], w=[vf.res])
                kb.dma(SP, nkp[s], kf.ap, r=[kf.res])
                kb.dma(SP, nvp[s], vf.ap, r=[vf.res])
        ws.done()
        bank = ps()
        bb = bank.ap.bitcast(BF16)

        def fnk(e):
            ins = None
            for j in range(4):
                ins = e.transpose(bb[:, j * 128:(j + 1) * 128], k_r.ap[:, j, :], ident.ap)
            return ins
        kb.op(PE, fnk, r=[k_r.res], w=[bank.res])
        s0 = (4 * t) % 8
        kb.op(ACT, lambda e: e.activation(out=V(KT.ap, s0 * 128, [[1, 512]]), in_=bb[:, 0:512], func=AF.Copy),
              r=[bank.res], w=[KTres[s0 + i] for i in range(4)])

        W0 = ws.next()
        W1 = ws.next()
        g = load_gb(0)
        QTs_, OTs_ = {}, {}

        def prep(j):
            bank = ps()
            bb = bank.ap.bitcast(BF16)

            def fnq(e):
                ins = None
                for gg in range(8):
                    ins = e.transpose(bb[:, gg * 128:(gg + 1) * 128], big4a[j].ap[:, gg * 128:(gg + 1) * 128],
                                      ident.ap)
                return ins
            kb.op(PE, fnq, r=[big4a[j].res], w=[bank.res])
            QT = B1024.next()
            kb.op(ACT, lambda e: e.activation(out=QT.ap, in_=bb, func=AF.Copy), r=[bank.res], w=[QT.res])
            QTs_[j] = QT
            OTs_[j] = B1024.next()

        def stageA(j, kv, half):
            n = 4 * t + j
            slot = n % 8
            pslot = (n - 1) % 8
            has_prev = n > 0
            QT = QTs_[j]
            pl = slice(kv * 64, (kv + 1) * 64)
            qs = QT.ap[pl, half * 512:(half + 1) * 512]
            Pp = None
            if has_prev:
                bSp = ps()
                mm_group(bSp.ap, [(V(KT.ap, pslot * 128, [[1, 128]], npart=64, pstart=kv * 64), qs)],
                         r=[KTres[pslot], QT.res], w=[bSp.res])
                Pp = B512.next()
                kb.op(ACT, lambda e: e.activation(out=Pp.ap, in_=bSp.ap, func=AF.Exp, scale=ATT_SCALE),
                      r=[bSp.res], w=[Pp.res])
                kb.op(DVE, lambda e: e.tensor_tensor(out=Pp.ap, in0=Pp.ap, in1=maskp.ap, op=ALU.mult),
                      r=[Pp.res], w=[Pp.res])
            bSc = ps()
            mm_group(bSc.ap, [(V(KT.ap, slot * 128, [[1, 128]], npart=64, pstart=kv * 64), qs)],
                     r=[KTres[slot], QT.res], w=[bSc.res])
            Pc = B512.next()
            kb.op(ACT, lambda e: e.activation(out=Pc.ap, in_=bSc.ap, func=AF.Exp, scale=ATT_SCALE),
                  r=[bSc.res], w=[Pc.res])
            kb.op(DVE, lambda e: e.tensor_tensor(out=Pc.ap, in0=Pc.ap, in1=maskc.ap, op=ALU.mult),
                  r=[Pc.res], w=[Pc.res])
            return (Pp, Pc)

        def stageB(j, kv, half, PP):
            n = 4 * t + j
            slot = n % 8
            pslot = (n - 1) % 8
            has_prev = n > 0
            Pp, Pc = PP
            Otok = OTs_[j]
            bO = ps()
            vcur = V(Vd.ap, slot * 256 + kv * 128, [[1, 65]])
            vprev = V(Vd.ap, pslot * 256 + kv * 128, [[1, 65]])

            def fnpv(e):
                ins = None
                for gg in range(4):
                    o_ap = bO.ap[:, gg * 128:gg * 128 + 65]
                    if has_prev:
                        ins = e.matmul(o_ap, Pp.ap[:, gg * 128:(gg + 1) * 128], vprev, start=True, stop=False)
                    ins = e.matmul(o_ap, Pc.ap[:, gg * 128:(gg + 1) * 128], vcur, start=(not has_prev), stop=True)
                return ins
            rr = [Vdres[slot], Pc.res] + ([Vdres[pslot], Pp.res] if has_prev else [])
            kb.op(PE, fnpv, r=rr, w=[bO.res])
            rd4 = rdp.next()
            h0 = kv * 8 + half * 4
            kb.op(DVE, lambda e: e.tensor_tensor(out=rd4.ap, in0=V(bO.ap, 64, [[128, 4]]),
                                                 in1=esink.ap[:, h0:h0 + 4], op=ALU.add),
                  r=[bO.res, esink.res], w=[rd4.res])
            kb.op(DVE, lambda e: e.reciprocal(rd4.ap, rd4.ap), r=[rd4.res], w=[rd4.res])
            kb.op(DVE, lambda e: e.tensor_tensor(out=V(Otok.ap, h0 * 64, [[64, 4], [1, 64]]),
                                                 in0=V(bO.ap, 0, [[128, 4], [1, 64]]),
                                                 in1=V(rd4.ap, 0, [[1, 4], [0, 64]]), op=ALU.mult),
                  r=[bO.res, rd4.res], w=[Otok.res])

        def post(j):
            Otok = OTs_[j]
            bank = ps()
            bb = bank.ap.bitcast(BF16)

            def fnt(e):
                ins = None
                for c in range(8):
                    ins = e.transpose(bb[:, c * 128:(c + 1) * 128], Otok.ap[:, c * 128:(c + 1) * 128], ident.ap)
                return ins
            kb.op(PE, fnt, r=[Otok.res], w=[bank.res])
            OT = B1024.next()
            kb.op(ACT, lambda e: e.activation(out=OT.ap, in_=bb, func=AF.Copy), r=[bank.res], w=[OT.res])
            for half, W in enumerate((W0, W1)):
                bank = ps()
                mm_group(bank.ap, [(OT.ap[:, c * 128:(c + 1) * 128], W.ap[:, c, :]) for c in range(8)],
                         r=[OT.res, W.res], w=[bank.res])
                resid_add(j, bank, half * 512, 512)
            layer_norm(j, g)

        units = [(j, kv, half) for j in range(4) for kv in range(2) for half in range(2)]
        prevu = None
        deferred = []
        for (j, kv, half) in units:
            if kv == 0 and half == 0:
                prep(j)
            PP = stageA(j, kv, half)
            if prevu is not None:
                pj, pkv, phalf, pPP = prevu
                stageB(pj, pkv, phalf, pPP)
                if deferred and not (pkv == 1 and phalf == 1) and phalf == 1:
                    make_xT(deferred.pop(0))
                if pkv == 1 and phalf == 1:
                    post(pj)
                    deferred.append(pj)
            prevu = (j, kv, half, PP)
        pj, pkv, phalf, pPP = prevu
        stageB(pj, pkv, phalf, pPP)
        while deferred:
            make_xT(deferred.pop(0))
        post(pj)
        make_xT(pj)
        ws.done()
        ws.done()

    def ffn_prompt(l, s, t, ln_idx, after=None, mid=None):
        cwl = cw[l]
        ap_ = aprev[l]
        if t == 0:
            kb.op(DVE, lambda e: e.memset(ap_.ap, 0.0), w=[ap_.res])
        g = load_gb(ln_idx)
        pend = []
        for si in range(11):
            W = ws.next()
            for ci in range(4):
                cc = si * 4 + ci
                bank = ps()
                mm_group(bank.ap, [(W.ap[:, kc, ci * 128:(ci + 1) * 128], xT.ap[:, kc, :]) for kc in range(8)],
                         r=xTr + [W.res], w=[bank.res])
                if cc < NCH:
                    c = cc
                    ab = abuf[c % 3]
                    abh = abufh[c % 3]
                    kb.op(ACT, lambda e: e.activation(out=ab.ap[:, 0:2], in_=ap_.ap[:, c, :], func=AF.Copy),
                          r=[ap_.res], w=[abh])
                    kb.op(ACT, lambda e: e.activation(out=ab.ap[:, 2:514], in_=bank.ap, func=AF.Copy),
                          r=[bank.res], w=[ab.res])
                    if len(pend) > 1:
                        pend.pop(0)()
                    c1 = F512.next()
                    c2 = F512.next()
                    kb.op(POOL, lambda e: e.tensor_scalar(c1.ap, ab.ap[:, 0:512], cwl.ap[:, c, 0:1],
                                                          cwl.ap[:, c, 3:4], op0=ALU.mult, op1=ALU.add),
                          r=[ab.res, abh], w=[c1.res])
                    kb.op(DVE, lambda e: e.scalar_tensor_tensor(out=c2.ap, in0=ab.ap[:, 1:513],
                                                                scalar=cwl.ap[:, c, 1:2], in1=c1.ap,
                                                                op0=ALU.mult, op1=ALU.add),
                          r=[ab.res, abh, c1.res], w=[c2.res])
                    kb.op(DVE, lambda e: e.scalar_tensor_tensor(out=c1.ap, in0=ab.ap[:, 2:514],
                                                                scalar=cwl.ap[:, c, 2:3], in1=c2.ap,
                                                                op0=ALU.mult, op1=ALU.add),
                          r=[ab.res, c2.res], w=[c1.res])
                    kb.op(ACT, lambda e: e.activation(out=ap_.ap[:, c, :], in_=ab.ap[:, 512:514], func=AF.Copy),
                          r=[ab.res], w=[ap_.res])

                    def gelu_later(c=c, c1=c1):
                        kb.op(ACT, lambda e: e.activation(out=gT.ap[:, c, :], in_=c1.ap, func=AF.Gelu),
                              r=[c1.res], w=[gTres[c]])
                    pend.append(gelu_later)
                else:
                    while pend:
                        pend.pop(0)()
                    c = cc - NCH
                    kb.op(DVE, lambda e: e.tensor_tensor(out=gT.ap[:, c, :], in0=gT.ap[:, c, :], in1=bank.ap,
                                                         op=ALU.mult),
                          r=[bank.res, gTres[c]], w=[gTres[c]])
            ws.done()
        if t == 3:
            for jj in range(2):
                kb.dma(SP, cpo[l, s, jj, :].rearrange("(c p o) -> p c o", p=128, o=1),
                       V(ap_.ap, jj, [[2, NCH], [1, 1]]), r=[ap_.res], allow_slow_non_contiguous=True)
        Wd = [ws.next() for _ in range(4)]
        for j in range(4):
            if j == 2 and mid is not None:
                mid()
            for qd in range(4):
                bank = ps()
                mm_group(bank.ap[:, 0:256],
                         [(gT.ap[:, c, j * 128:(j + 1) * 128], Wd[qd].ap[:, c, :]) for c in range(NCH)],
                         r=gTres + [Wd[qd].res], w=[bank.res])
                resid_add(j, bank, qd * 256, 256)
            layer_norm(j, g)
            if j >= 1 and after is not None:
                after(j - 1)
        for _ in range(4):
            ws.done()
        if after is not None:
            after(3)

    def gla_prompt(s, t):
        Wq = ws.next()
        Wk = ws.next()
        if t == 0:
            kb.op(DVE, lambda e: e.memset(Sst.ap, 0.0), w=[Sst.res])
        for j in range(4):
            js = slice(j * 128, (j + 1) * 128)
            bank = ps()
            mm_group(bank.ap[0:16, 0:128], [(wa1_b.ap[:, kc, :], xT.ap[:, kc, js]) for kc in range(8)],
                     r=[xTr[j]], w=[bank.res])
            kb.op(ACT, lambda e: e.activation(out=rT_b.ap[:, js], in_=bank.ap[0:16, 0:128], func=AF.Copy),
                  r=[bank.res], w=[rT_b.res])
            bz = ps()
            mm_group(bz.ap, [(rT_b.ap[:, js], wa2_b.ap), (ones1.ap[0:1, :], ba_b.ap[0:1, :])],
                     r=[rT_b.res, ones1.res], w=[bz.res])
            e1 = F512.next()
            sp = F512.next()
            kb.op(ACT, lambda e: e.activation(out=e1.ap, in_=bz.ap, func=AF.Exp, scale=-1.0),
                  r=[bz.res], w=[e1.res])
            kb.op(ACT, lambda e: e.activation(out=sp.ap, in_=e1.ap, func=AF.Ln, bias=one_c.ap, scale=1.0),
                  r=[e1.res, one_c.res], w=[sp.res])
            bb_ = ps()

            def fnb(e, sp=sp, bb_=bb_):
                ins = None
                for h in range(4):
                    ins = e.matmul(bb_.ap[:, h * 128:(h + 1) * 128], sp.ap[:, h * 128:(h + 1) * 128], utri.ap,
                                   start=True, stop=True)
                return ins
            kb.op(PE, fnb, r=[sp.res], w=[bb_.res])
            bd = ps()
            mm_group(bd.ap, [(lstr.ap, sp.ap)], r=[sp.res], w=[bd.res])
            eb = F512.next()
            enb = F512.next()
            dec = F512.next()
            kb.op(ACT, lambda e: e.activation(out=eb.ap, in_=bb_.ap, func=AF.Exp), r=[bb_.res], w=[eb.res])
            kb.op(ACT, lambda e: e.activation(out=enb.ap, in_=bb_.ap, func=AF.Exp, scale=-1.0),
                  r=[bb_.res], w=[enb.res])
            kb.op(ACT, lambda e: e.activation(out=dec.ap, in_=bd.ap, func=AF.Exp), r=[bd.res], w=[dec.res])
            kb.op(DVE, lambda e: e.tensor_copy(ebl.ap[:, j, :], V(eb.ap, 127, [[128, 4]])),
                  r=[eb.res], w=[ebl.res])
            for (Wx, dstT, mult) in ((Wq, qinT, eb), (Wk, kinT, enb)):
                bq = ps()

                def fnq(e, Wx=Wx, bq=bq, js=js):
                    ins = None
                    for h in range(4):
                        for kc in range(8):
                            ins = e.matmul(bq.ap[:, h * 128:(h + 1) * 128], Wx.ap[:, kc, h * 128:(h + 1) * 128],
                                           xT.ap[:, kc, js], start=(kc == 0), stop=(kc == 7))
                    return ins
                kb.op(PE, fnq, r=[xTr[j]] + [Wx.res], w=[bq.res])
                sc_ = DKS if dstT is qinT else 1.0
                kb.op(DVE, lambda e: e.scalar_tensor_tensor(out=V(dstT.ap, j * 128, [[512, 4], [1, 128]]),
                                                            in0=V(bq.ap, 0, [[128, 4], [1, 128]]), scalar=sc_,
                                                            in1=V(mult.ap, 0, [[128, 4], [1, 128]]),
                                                            op0=ALU.mult, op1=ALU.mult),
                      r=[bq.res, mult.res], w=[dstT.res])
            bkt = ps()
            mm_group(bkt.ap, [(xT.ap[:, kc, js], Wk.ap[:, kc, :]) for kc in range(8)],
                     r=[xTr[j]] + [Wk.res], w=[bkt.res])
            kb.op(DVE, lambda e: e.tensor_tensor(out=kdec[j].ap, in0=bkt.ap, in1=dec.ap, op=ALU.mult),
                  r=[bkt.res, dec.res], w=[kdec[j].res])
        ws.done()
        ws.done()
        for vi in range(2):
            W = ws.next()
            for j in range(4):
                bank = ps()
                mm_group(bank.ap, [(xT.ap[:, kc, j * 128:(j + 1) * 128], W.ap[:, kc, :]) for kc in range(8)],
                         r=[xTr[j]] + [W.res], w=[bank.res])
                kb.op(ACT, lambda e: e.activation(out=big4a[j].ap[:, vi * 512:(vi + 1) * 512], in_=bank.ap,
                                                  func=AF.Copy), r=[bank.res], w=[big4a[j].res])
            ws.done()
        for gi in range(2):
            W = ws.next()
            for j in range(4):
                bank = ps()
                mm_group(bank.ap, [(xT.ap[:, kc, j * 128:(j + 1) * 128], W.ap[:, kc, :]) for kc in range(8)],
                         r=[xTr[j]] + [W.res], w=[bank.res])
                kb.op(ACT, lambda e: e.activation(out=big4b[j].ap[:, gi * 512:(gi + 1) * 512], in_=bank.ap,
                                                  func=AF.Silu), r=[bank.res], w=[big4b[j].res])
            ws.done()
        W0 = ws.next()
        W1 = ws.next()
        g = load_gb(2)
        OGB = {}

        ATT, BOS = {}, {}

        def sa(j):
            n = 4 * t + j
            vt = big4a[j]
            for hp in range(2):
                bU = ps()

                def fnu(e, bU=bU, hp=hp, vt=vt, j=j):
                    ins = None
                    for h in (2 * hp, 2 * hp + 1):
                        ins = e.matmul(bU.ap[:, (h % 2) * 256:(h % 2 + 1) * 256],
                                       kdec[j].ap[:, h * 128:(h + 1) * 128], vt.ap[:, h * 256:(h + 1) * 256],
                                       start=True, stop=True)
                    return ins
                kb.op(PE, fnu, r=[kdec[j].res, vt.res], w=[bU.res])
                for h in (2 * hp, 2 * hp + 1):
                    kb.op(DVE, lambda e: e.scalar_tensor_tensor(
                        out=Sst.ap[:, h, :], in0=Sst.ap[:, h, :], scalar=ebl.ap[:, j, h:h + 1],
                        in1=bU.ap[:, (h % 2) * 256:(h % 2 + 1) * 256], op0=ALU.mult, op1=ALU.add),
                        r=[bU.res, Sst.res, ebl.res], w=[Sst.res])
            if n == 15:
                kb.dma(SP, gp[s].rearrange("h d v -> d h v"), Sst.ap, r=[Sst.res])
            else:
                sb_ = Sbf2[n % 2]
                kb.op(ACT, lambda e: e.activation(out=sb_.ap, in_=Sst.ap, func=AF.Copy),
                      r=[Sst.res], w=[sb_.res])

        def sb(j):
            js = slice(j * 128, (j + 1) * 128)
            bA = ps()

            def fna(e, bA=bA, js=js):
                ins = None
                for h in range(4):
                    ins = e.matmul(bA.ap[:, h * 128:(h + 1) * 128], kinT.ap[:, h, js], qinT.ap[:, h, js],
                                   start=True, stop=True)
                return ins
            kb.op(PE, fna, r=[kinT.res, qinT.res], w=[bA.res])
            att = B512.next()
            kb.op(DVE, lambda e: e.tensor_tensor(out=att.ap, in0=bA.ap, in1=trim.ap, op=ALU.mult),
                  r=[bA.res], w=[att.res])
            ATT[j] = att

        def sc1(j):
            n = 4 * t + j
            first = n == 0
            js = slice(j * 128, (j + 1) * 128)
            vt = big4a[j]
            att = ATT[j]
            sprev = Sbf2[(n - 1) % 2]
            st = stat[j % 2]
            kb.op(DVE, lambda e: e.memset(st.ap[:, 16:20], 0.0), w=[st.res])
            bOs = []
            for hp in range(2):
                bO = ps()
                bOs.append(bO)

                def fno(e, bO=bO, hp=hp, att=att, vt=vt, js=js, first=first):
                    ins = None
                    for h in (2 * hp, 2 * hp + 1):
                        o_ap = bO.ap[:, (h % 2) * 256:(h % 2 + 1) * 256]
                        ins = e.matmul(o_ap, att.ap[:, h * 128:(h + 1) * 128], vt.ap[:, h * 256:(h + 1) * 256],
                                       start=True, stop=first)
                        if not first:
                            ins = e.matmul(o_ap, qinT.ap[:, h, js], sprev.ap[:, h, :], start=False, stop=True)
                    return ins
                kb.op(PE, fno, r=[att.res, vt.res, qinT.res, sprev.res], w=[bO.res])
                for h in (2 * hp, 2 * hp + 1):
                    junk = F512.next()
                    kb.op(ACT, lambda e: e.activation(out=junk.ap[:, 0:256],
                                                      in_=bO.ap[:, (h % 2) * 256:(h % 2 + 1) * 256],
                                                      func=AF.Square, accum_out=st.ap[:, 16 + h:17 + h]),
                          r=[bO.res], w=[junk.res, st.res])
            kb.op(ACT, lambda e: e.activation(out=st.ap[:, 20:24], in_=st.ap[:, 16:20], func=AF.Ln,
                                              bias=eps_nm.ap, scale=1.0 / 256.0), r=[st.res, eps_nm.res],
                  w=[st.res])
            kb.op(ACT, lambda e: e.activation(out=st.ap[:, 20:24], in_=st.ap[:, 20:24], func=AF.Exp, scale=-0.5),
                  r=[st.res], w=[st.res])
            BOS[j] = bOs

        def sc2(j):
            st = stat[j % 2]
            bOs = BOS[j]
            ogb = B1024.next()
            for hp in range(2):
                ogt = F512.next()
                for h in (2 * hp, 2 * hp + 1):
                    kb.op(DVE, lambda e: e.scalar_tensor_tensor(
                        out=ogt.ap[:, (h % 2) * 256:(h % 2 + 1) * 256],
                        in0=bOs[hp].ap[:, (h % 2) * 256:(h % 2 + 1) * 256], scalar=st.ap[:, 20 + h:21 + h],
                        in1=ng_rep.ap, op0=ALU.mult, op1=ALU.mult), r=[bOs[hp].res, st.res], w=[ogt.res])
                kb.op(DVE, lambda e: e.tensor_tensor(out=ogb.ap[:, hp * 512:(hp + 1) * 512], in0=ogt.ap,
                                                     in1=big4b[j].ap[:, hp * 512:(hp + 1) * 512], op=ALU.mult),
                      r=[ogt.res, big4b[j].res], w=[ogb.res])
            OGB[j] = ogb

        def s2(j):
            ogb = OGB[j]
            bank = ps()
            bb = bank.ap.bitcast(BF16)

            def fnt(e, bb=bb, ogb=ogb):
                ins = None
                for c in range(8):
                    ins = e.transpose(bb[:, c * 128:(c + 1) * 128], ogb.ap[:, c * 128:(c + 1) * 128], ident.ap)
                return ins
            kb.op(PE, fnt, r=[ogb.res], w=[bank.res])
            ogT = B1024.next()
            kb.op(ACT, lambda e: e.activation(out=ogT.ap, in_=bb, func=AF.Copy), r=[bank.res], w=[ogT.res])
            for half, W in enumerate((W0, W1)):
                bank = ps()
                mm_group(bank.ap, [(ogT.ap[:, c * 128:(c + 1) * 128], W.ap[:, c, :]) for c in range(8)],
                         r=[ogT.res, W.res], w=[bank.res])
                resid_add(j, bank, half * 512, 512)
            layer_norm(j, g)

        for f_, a_ in ((sa, 0), (sb, 0), (sc1, 0), (sa, 1), (sb, 1), (sc2, 0), (sc1, 1), (sa, 2), (sb, 2),
                       (sc2, 1), (s2, 0), (sc1, 2), (sa, 3), (sb, 3), (sc2, 2), (s2, 1), (make_xT, 0),
                       (sc1, 3), (sc2, 3), (s2, 2), (make_xT, 1), (s2, 3), (make_xT, 2), (make_xT, 3)):
            f_(a_)
        ws.done()
        ws.done()

    NS = NSMP

    SLV = float(os.environ.get("MK_SLV", "99"))

    def attn_sample():
        g = load_gb(0)
        if SLV < 1:
            return
        KV_sb = [(xtok[1], xtok[2]), (xtok[3], T(V(Sst.ap, 0, [[1, 1024]]), Sst.res))]
        for g8 in range(2):
            bsl = slice(8 * g8, 8 * g8 + 8)
            for (src_c, sb_) in ((ck, KV_sb[g8][0]), (cv, KV_sb[g8][1])):
                sl_ = kb.newslot("kvsw_" + sb_.res.name)
                kb.dma(POOL, V(sb_.ap, 0, [[128, 8], [1, 128]], npart=64),
                       src_c[bsl, 1:65, :].rearrange("b s c -> s b c"), w=[sb_.res], slot=sl_)
                kb.dma(POOL, V(sb_.ap, 0, [[128, 8], [1, 128]], npart=63, pstart=64),
                       src_c[bsl, 65:128, :].rearrange("b s c -> s b c"), w=[sb_.res], slot=sl_)
        for si in range(2):
            W = ws.next()
            bank = ps()
            pairs = [(xT.ap[:, kc, 0:NS], W.ap[:, kc, :]) for kc in range(8)]
            pairs.append((ones1.ap[0:1, 0:NS], bq_b.ap[0:1, si * 512:(si + 1) * 512]))
            mm_group(bank.ap[0:NS, :], pairs, r=[xTr[0]] + [W.res, ones1.res], w=[bank.res])
            dst = V(big4a[0].ap, si * 64, [[128, 8], [1, 64]], npart=NS)
            rope((bank.ap[0:NS, :], bank.res), 8, (dst, [big4a[0].res]), ropecs.ap, ropess.ap, ntok=NS)
            ws.done()
        W = ws.next()
        bank = ps()
        pairs = [(xT.ap[:, kc, 0:NS], W.ap[:, kc, :]) for kc in range(8)]
        pairs.append((ones1.ap[0:1, 0:NS], bq_b.ap[0:1, 1024:1280]))
        mm_group(bank.ap[0:NS, 0:256], pairs, r=[xTr[0]] + [W.res, ones1.res], w=[bank.res])
        rope((bank.ap[0:NS, 0:128], bank.res), 2, (V(ksm.ap, 0, [[64, 2], [1, 64]]), [ksm.res]),
             ropecs.ap, ropess.ap, ntok=NS)
        kb.op(ACT, lambda e: e.activation(out=vsm.ap, in_=bank.ap[0:NS, 128:256], func=AF.Copy),
              r=[bank.res], w=[vsm.res])
        ws.done()
        if SLV < 2:
            return
        bank = ps()
        bb = bank.ap.bitcast(BF16)

        def fnq(e):
            ins = None
            for gg in range(8):
                ins = e.transpose(bb[:, gg * NS:(gg + 1) * NS], big4a[0].ap[0:NS, gg * 128:(gg + 1) * 128],
                                  ident.ap[0:NS, 0:NS])
            return ins
        kb.op(PE, fnq, r=[big4a[0].res], w=[bank.res])
        QTs = B512.next()
        kb.op(DVE, lambda e: e.memset(QTs.ap[:, 0:256], 0.0), w=[QTs.res])
        for kv in range(2):
            kb.op(ACT, lambda e: e.activation(out=V(QTs.ap, kv * 8, [[1, 8], [16, NS]], npart=64, pstart=kv * 64),
                                              in_=V(bb, 0, [[NS, 8], [1, NS]], npart=64, pstart=kv * 64),
                                              func=AF.Copy), r=[bank.res], w=[QTs.res])
        W0 = ws.next()
        W1 = ws.next()
        for g8 in range(2):
            if SLV < 3:
                continue
            bsl = slice(8 * g8, 8 * g8 + 8)
            Ksb, Vsb = KV_sb[g8]
            for (src_c, sb_, sm, outd) in ((ck, Ksb, ksm, nks), (cv, Vsb, vsm, nvs)):
                kb.dma(SP, V(sb_.ap, 0, [[128, 8], [1, 128]], npart=1, pstart=127), sm.ap[bsl, :],
                       r=[sm.res], w=[sb_.res])
                kb.dma(SP, outd[bsl].rearrange("b s c -> s b c"), V(sb_.ap, 0, [[128, 8], [1, 128]]),
                       r=[sb_.res])
            if SLV < 4:
                continue
            Kb = B1024.next()
            Vdk = [B1024.next(), B1024.next()]
            kb.op(ACT, lambda e: e.activation(out=Kb.ap, in_=Ksb.ap, func=AF.Copy), r=[Ksb.res], w=[Kb.res])
            for kv in range(2):
                kb.op(ACT, lambda e: e.activation(out=V(Vdk[kv].ap, 0, [[128, 8], [64, 2], [1, 64]]),
                                                  in_=V(Vsb.ap, kv * 64, [[128, 8], [0, 2], [1, 64]]),
                                                  func=AF.Copy), r=[Vsb.res], w=[Vdk[kv].res])
            bank = ps()
            bb = bank.ap.bitcast(BF16)

            def fnk(e, bb=bb, Kb=Kb):
                ins = None
                for bl in range(8):
                    ins = e.transpose(bb[:, bl * 128:(bl + 1) * 128], Kb.ap[:, bl * 128:(bl + 1) * 128], ident.ap)
                return ins
            kb.op(PE, fnk, r=[Kb.res], w=[bank.res])
            KTs = B1024.next()
            kb.op(ACT, lambda e: e.activation(out=KTs.ap, in_=bb, func=AF.Copy), r=[bank.res], w=[KTs.res])
            if SLV < 4.2:
                continue
            bS = ps()

            def fns(e, bS=bS, KTs=KTs, g8=g8):
                ins = None
                for bl in range(8):
                    b = 8 * g8 + bl
                    ins = e.matmul(bS.ap[:, bl * 16:(bl + 1) * 16], KTs.ap[:, bl * 128:(bl + 1) * 128],
                                   QTs.ap[:, b * 16:(b + 1) * 16], start=True, stop=True)
                return ins
            kb.op(PE, fns, r=[KTs.res, QTs.res], w=[bS.res])
            Ps = B512.next()
            kb.op(ACT, lambda e: e.activation(out=Ps.ap[:, 0:128], in_=bS.ap[:, 0:128], func=AF.Exp,
                                              scale=ATT_SCALE), r=[bS.res], w=[Ps.res])
            if SLV < 4.3:
                continue
            bO = ps()

            def fno(e, bO=bO, Vdk=Vdk, Ps=Ps):
                ins = None
                for bl in range(8):
                    for kv in range(2):
                        co = bl * 16 + kv * 8
                        ins = e.matmul(bO.ap[:, co:co + 8], Vdk[kv].ap[:, bl * 128:(bl + 1) * 128],
                                       Ps.ap[:, co:co + 8], start=True, stop=True)
                return ins
            kb.op(PE, fno, r=[Vdk[0].res, Vdk[1].res, Ps.res], w=[bO.res])
            bD = ps()
            mm_group(bD.ap[:, 0:128], [(ones128.ap, Ps.ap[:, 0:128])], r=[ones128.res, Ps.res], w=[bD.res])
            if SLV < 4.4:
                continue
            rd = F512.next()
            kb.op(DVE, lambda e: e.tensor_tensor(out=V(rd.ap, 0, [[16, 8], [1, 16]]),
                                                 in0=V(bD.ap, 0, [[16, 8], [1, 16]]),
                                                 in1=V(esink.ap, 0, [[0, 8], [1, 16]]), op=ALU.add),
                  r=[bD.res, esink.res], w=[rd.res])
            kb.op(DVE, lambda e: e.reciprocal(rd.ap[:, 0:128], rd.ap[:, 0:128]), r=[rd.res], w=[rd.res])
            if SLV < 4.5:
                continue
            for kv in range(2):
                for par in range(2):
                    kb.op(DVE, lambda e: e.tensor_tensor(
                        out=V(ots.ap, kv * 64 + g8 * 8, [[16, 4], [1, 8]], npart=64, pstart=par * 64),
                        in0=V(bO.ap, kv * 8 + par, [[2, 4], [16, 8]], npart=64, pstart=par * 64),
                        in1=V(rd.ap, kv * 8 + par, [[2, 4], [16, 8]], npart=64, pstart=par * 64),
                        op=ALU.mult), r=[bO.res, rd.res], w=[ots.res])
        if SLV < 5:
            return
        for half, W in enumerate((W0, W1)):
            bank = ps()
            mm_group(bank.ap[0:NS, :], [(ots.ap[:, c, :], W.ap[:, c, :]) for c in range(8)],
                     r=[ots.res, W.res], w=[bank.res])
            resid_add(0, bank, half * 512, 512, ntok=NS)
        layer_norm(0, g, NS)
        make_xT(0, NS)
        ws.done()
        ws.done()

    def ffn_sample(l, ln_idx):
        cwl = cw[l]
        g = load_gb(ln_idx)
        kb.dma(SP, cso[l, :, 0, :], sc[l, :, 1, :])
        for jj in range(2):
            for p in range(6):
                wd = min(512, DFF - p * 512)
                nchk = wd // 128
                tb = F512.next()
                kb.dma(SP, tb.ap[0:NS, 0:wd], sc[l, :, jj, p * 512:p * 512 + wd], w=[tb.res])
                bank = ps()

                def fntr(e, tb=tb, bank=bank, nchk=nchk):
                    ins = None
                    for ci in range(nchk):
                        ins = e.transpose(bank.ap[:, ci * NS:(ci + 1) * NS], tb.ap[0:NS, ci * 128:(ci + 1) * 128],
                                          ident_f.ap[0:NS, 0:NS])
                    return ins
                kb.op(PE, fntr, r=[tb.res], w=[bank.res])
                kb.op(ACT, lambda e: e.activation(out=V(cst.ap, p * 4 * 32 + jj * 16, [[32, nchk], [1, NS]]),
                                                  in_=V(bank.ap, 0, [[NS, nchk], [1, NS]]), func=AF.Copy),
                      r=[bank.res], w=[cst.res])
        for si in range(11):
            W = ws.next()
            bank = ps()

            def fnu(e, W=W, bank=bank):
                ins = None
                for ci in range(4):
                    for kc in range(8):
                        ins = e.matmul(bank.ap[:, ci * NS:(ci + 1) * NS], W.ap[:, kc, ci * 128:(ci + 1) * 128],
                                       xT.ap[:, kc, 0:NS], start=(kc == 0), stop=(kc == 7))
                return ins
            kb.op(PE, fnu, r=xTr + [W.res], w=[bank.res])
            kb.op(ACT, lambda e: e.activation(out=hsm.ap[:, si * 64:(si + 1) * 64], in_=bank.ap[:, 0:64],
                                              func=AF.Copy), r=[bank.res], w=[hsm.res])
            ws.done()
        NA = NCH * NS
        t1 = F512.next()
        t2 = F512.next()
        wv = lambda k: V(cwl.ap, k, [[4, NCH], [0, NS]])
        v3 = lambda ap, off=0: V(ap, off, [[NS, NCH], [1, NS]])
        kb.op(DVE, lambda e: e.tensor_tensor(out=v3(t1.ap), in0=V(cst.ap, 0, [[32, NCH], [1, NS]]), in1=wv(0),
                                             op=ALU.mult), r=[cst.res], w=[t1.res])
        kb.op(DVE, lambda e: e.tensor_tensor(out=v3(t2.ap), in0=V(cst.ap, 16, [[32, NCH], [1, NS]]), in1=wv(1),
                                             op=ALU.mult), r=[cst.res], w=[t2.res])
        kb.op(DVE, lambda e: e.tensor_tensor(out=t1.ap[:, 0:NA], in0=t1.ap[:, 0:NA], in1=t2.ap[:, 0:NA],
                                             op=ALU.add), r=[t1.res, t2.res], w=[t1.res])
        kb.op(DVE, lambda e: e.tensor_tensor(out=v3(t2.ap), in0=v3(hsm.ap), in1=wv(2), op=ALU.mult),
              r=[hsm.res, t2.res], w=[t2.res])
        kb.op(DVE, lambda e: e.tensor_tensor(out=t1.ap[:, 0:NA], in0=t1.ap[:, 0:NA], in1=t2.ap[:, 0:NA],
                                             op=ALU.add), r=[t1.res, t2.res], w=[t1.res])
        kb.op(DVE, lambda e: e.tensor_tensor(out=v3(t1.ap), in0=v3(t1.ap), in1=wv(3), op=ALU.add),
              r=[t1.res], w=[t1.res])
        kb.op(ACT, lambda e: e.activation(out=t2.ap[:, 0:NA], in_=t1.ap[:, 0:NA], func=AF.Gelu),
              r=[t1.res, t2.res], w=[t2.res])
        kb.op(DVE, lambda e: e.tensor_tensor(out=V(gsm.ap, 0, [[1, NA]]), in0=t2.ap[:, 0:NA],
                                             in1=hsm.ap[:, NA:2 * NA], op=ALU.mult),
              r=[t2.res, hsm.res], w=[gsm.res])
        for p in range(6):
            wd = min(512, DFF - p * 512)
            nchk = wd // 128
            bank = ps()

            def fnts(e, bank=bank, nchk=nchk, p=p):
                ins = None
                for ci in range(nchk):
                    c = p * 4 + ci
                    ins = e.transpose(bank.ap[0:NS, ci * 128:(ci + 1) * 128], hsm.ap[:, c * NS:(c + 1) * NS],
                                      ident_f.ap)
                return ins
            kb.op(PE, fnts, r=[hsm.res], w=[bank.res])
            tb = F512.next()
            kb.op(ACT, lambda e: e.activation(out=tb.ap[0:NS, 0:wd], in_=bank.ap[0:NS, 0:wd], func=AF.Copy),
                  r=[bank.res], w=[tb.res])
            kb.dma(SP, cso[l, :, 1, p * 512:p * 512 + wd], tb.ap[0:NS, 0:wd], r=[tb.res])
        for qd in range(4):
            W = ws.next()
            bank = ps()
            mm_group(bank.ap[0:NS, 0:256], [(gsm.ap[:, c, :], W.ap[:, c, :]) for c in range(NCH)],
                     r=[gsm.res, W.res], w=[bank.res])
            resid_add(0, bank, qd * 256, 256, ntok=NS)
            ws.done()
        layer_norm(0, g, NS)

    def gla_sample():
        Wq = ws.next()
        Wk = ws.next()
        g = load_gb(2)
        kb.op(DVE, lambda e: e.tensor_scalar(nbaT.ap, baT.ap, -1.0, None, op0=ALU.mult), w=[nbaT.res])
        bank = ps()
        mm_group(bank.ap[0:16, 0:NS], [(wa1_b.ap[:, kc, :], xT.ap[:, kc, 0:NS]) for kc in range(8)],
                 r=xTr, w=[bank.res])
        kb.op(ACT, lambda e: e.activation(out=rT_b.ap[:, 0:NS], in_=bank.ap[0:16, 0:NS], func=AF.Copy),
              r=[bank.res], w=[rT_b.res])
        bz = ps()

        def fnz(e):
            ins = None
            for h in range(4):
                ins = e.matmul(bz.ap[:, h * NS:(h + 1) * NS], wa2_b.ap[:, h * 128:(h + 1) * 128],
                               rT_b.ap[:, 0:NS], start=True, stop=True)
            return ins
        kb.op(PE, fnz, r=[rT_b.res], w=[bz.res])
        e1 = smallf.ap[:, 3, :]
        eaT = smallf.ap[:, 0, :]
        qts = smallf.ap[:, 1, :]
        kts = smallf.ap[:, 2, :]
        for h in range(4):
            kb.op(ACT, lambda e: e.activation(out=e1[:, h * NS:(h + 1) * NS], in_=bz.ap[:, h * NS:(h + 1) * NS],
                                              func=AF.Exp, scale=-1.0, bias=nbaT.ap[:, h:h + 1]),
                  r=[bz.res, nbaT.res], w=[smallf.res])
        kb.op(ACT, lambda e: e.activation(out=e1, in_=e1, func=AF.Ln, bias=one_c.ap, scale=1.0),
              r=[smallf.res, one_c.res], w=[smallf.res])
        kb.op(ACT, lambda e: e.activation(out=eaT, in_=e1, func=AF.Exp, scale=-1.0 / 16.0),
              r=[smallf.res], w=[smallf.res])
        for (Wx, dst_, sc_) in ((Wq, qts, DKS), (Wk, kts, 1.0)):
            bq = ps()

            def fnq(e, Wx=Wx, bq=bq):
                ins = None
                for h in range(4):
                    for kc in range(8):
                        ins = e.matmul(bq.ap[:, h * NS:(h + 1) * NS], Wx.ap[:, kc, h * 128:(h + 1) * 128],
                                       xT.ap[:, kc, 0:NS], start=(kc == 0), stop=(kc == 7))
                return ins
            kb.op(PE, fnq, r=[xTr[0]] + [Wx.res], w=[bq.res])
            kb.op(DVE, lambda e: e.tensor_scalar(dst_, bq.ap[:, 0:64], sc_, None, op0=ALU.mult),
                  r=[bq.res], w=[smallf.res])
        ws.done()
        ws.done()
        for (dstb, fn_) in ((big4a[0], AF.Copy), (big4b[0], AF.Silu)):
            for vi in range(2):
                W = ws.next()
                bank = ps()
                mm_group(bank.ap[0:NS, :], [(xT.ap[:, kc, 0:NS], W.ap[:, kc, :]) for kc in range(8)],
                         r=[xTr[0]] + [W.res], w=[bank.res])
                kb.op(ACT, lambda e: e.activation(out=dstb.ap[0:NS, vi * 512:(vi + 1) * 512], in_=bank.ap[0:NS, :],
                                                  func=fn_), r=[bank.res], w=[dstb.res])
                ws.done()
        W0 = ws.next()
        W1 = ws.next()
        qm = big4a[1]
        kb.op(DVE, lambda e: e.tensor_tensor(out=V(qm.ap, 0, [[256, 4], [16, 16], [1, 16]]),
                                             in0=V(qts, 0, [[16, 4], [0, 16], [1, 16]]),
                                             in1=V(eye_rep.ap, 0, [[0, 4], [16, 16], [1, 16]]), op=ALU.mult),
              r=[smallf.res], w=[qm.res])
        bo = [ps_reserve(), ps_reserve()]
        S0b = [xtok[1], xtok[2]]
        Snb = [xtok[3], T(V(Sst.ap, 0, [[1, 1024]]), Sst.res)]
        Sbb = [Sbf, T(V(kdec[0].ap, 0, [[1, 512]]), kdec[0].res)]
        Sbb2 = [kdec[1], kdec[2]]
        v4 = lambda ap: V(ap, 0, [[256, 4], [1, 256]])
        for b in range(NS):
            S0 = S0b[b % 2]
            Sn = Snb[b % 2]
            if b == 0 and not s0_prefetched:
                for b2 in range(2):
                    kb.dma(SP, v4(S0b[b2].ap), sg[b2].rearrange("h d v -> d h v"), w=[S0b[b2].res])
            vm = B1024.next()
            kb.op(DVE, lambda e: e.tensor_scalar(vm.ap[0:NS, :], big4a[0].ap[0:NS, :], eye16.ap[:, b:b + 1], None,
                                                 op0=ALU.mult), r=[big4a[0].res], w=[vm.res])
            Sb_lo = Sbb[b % 2] if False else None
            sbf_t = B1024.next()
            for h in range(4):
                bv = ps()
                mm_group(bv.ap[:, 0:256], [(ones128.ap[0:NS, :], vm.ap[0:NS, h * 256:(h + 1) * 256])],
                         r=[ones128.res, vm.res], w=[bv.res])
                tmp = F512.next()
                col = h * NS + b
                kb.op(POOL, lambda e: e.tensor_scalar(tmp.ap[:, 0:256], S0.ap[:, h * 256:(h + 1) * 256],
                                                      eaT[:, col:col + 1], None, op0=ALU.mult),
                      r=[S0.res, smallf.res], w=[tmp.res])
                kb.op(DVE, lambda e: e.scalar_tensor_tensor(out=Sn.ap[:, h * 256:(h + 1) * 256],
                                                            in0=bv.ap[:, 0:256], scalar=kts[:, col:col + 1],
                                                            in1=tmp.ap[:, 0:256], op0=ALU.mult, op1=ALU.add),
                      r=[bv.res, tmp.res, smallf.res], w=[Sn.res])
            if b + 2 < NS:
                kb.dma(SP, v4(S0.ap), sg[b + 2].rearrange("h d v -> d h v"), w=[S0.res])
            kb.op(ACT, lambda e: e.activation(out=sbf_t.ap, in_=V(Sn.ap, 0, [[1, 1024]]), func=AF.Copy),
                  r=[Sn.res], w=[sbf_t.res])
            kb.dma(SP, gs[b].rearrange("h d v -> d h v"), v4(Sn.ap), r=[Sn.res])

            def fno(e, b=b, sbf_t=sbf_t):
                ins = None
                for h in range(4):
                    ins = e.matmul(bo[h // 2].ap[0:NS, (h % 2) * 256:(h % 2 + 1) * 256],
                                   V(qm.ap, h * 256 + b * 16, [[1, 16]]), sbf_t.ap[:, h * 256:(h + 1) * 256],
                                   start=(b == 0 and h % 2 == 0), stop=(b == NS - 1), skip_group_check=True)
                return ins
            kb.op(PE, fno, r=[qm.res, sbf_t.res], w=[bo[0].res, bo[1].res])
        st = stat[0]
        kb.op(DVE, lambda e: e.memset(st.ap[:, 16:20], 0.0), w=[st.res])
        for h in range(4):
            junk = F512.next()
            kb.op(ACT, lambda e: e.activation(out=junk.ap[0:NS, 0:256],
                                              in_=bo[h // 2].ap[0:NS, (h % 2) * 256:(h % 2 + 1) * 256],
                                              func=AF.Square, accum_out=st.ap[0:NS, 16 + h:17 + h]),
                  r=[bo[h // 2].res], w=[junk.res, st.res])
        kb.op(ACT, lambda e: e.activation(out=st.ap[0:NS, 20:24], in_=st.ap[0:NS, 16:20], func=AF.Ln,
                                          bias=eps_nm.ap[0:NS, :], scale=1.0 / 256.0), r=[st.res, eps_nm.res],
              w=[st.res])
        kb.op(ACT, lambda e: e.activation(out=st.ap[0:NS, 20:24], in_=st.ap[0:NS, 20:24], func=AF.Exp,
                                          scale=-0.5), r=[st.res], w=[st.res])
        ogb = B1024.next()
        for hp in range(2):
            ogt = F512.next()
            for h in (2 * hp, 2 * hp + 1):
                kb.op(DVE, lambda e: e.scalar_tensor_tensor(
                    out=ogt.ap[0:NS, (h % 2) * 256:(h % 2 + 1) * 256],
                    in0=bo[hp].ap[0:NS, (h % 2) * 256:(h % 2 + 1) * 256], scalar=st.ap[0:NS, 20 + h:21 + h],
                    in1=ng_rep.ap[0:NS, :], op0=ALU.mult, op1=ALU.mult), r=[bo[hp].res, st.res], w=[ogt.res])
            kb.op(DVE, lambda e: e.tensor_tensor(out=ogb.ap[0:NS, hp * 512:(hp + 1) * 512], in0=ogt.ap[0:NS, :],
                                                 in1=big4b[0].ap[0:NS, hp * 512:(hp + 1) * 512], op=ALU.mult),
                  r=[ogt.res, big4b[0].res], w=[ogb.res])
        ps_release(bo[0])
        ps_release(bo[1])
        bank = ps()
        bb = bank.ap.bitcast(BF16)

        def fnt(e):
            ins = None
            for c in range(8):
                ins = e.transpose(bb[:, c * NS:(c + 1) * NS], ogb.ap[0:NS, c * 128:(c + 1) * 128],
                                  ident.ap[0:NS, 0:NS])
            return ins
        kb.op(PE, fnt, r=[ogb.res], w=[bank.res])
        kb.op(ACT, lambda e: e.activation(out=V(ogTs.ap, 0, [[1, 128]]), in_=bb[:, 0:128], func=AF.Copy),
              r=[bank.res], w=[ogTs.res])
        for half, W in enumerate((W0, W1)):
            bank = ps()
            mm_group(bank.ap[0:NS, :], [(ogTs.ap[:, c, :], W.ap[:, c, :]) for c in range(8)],
                     r=[ogTs.res, W.res], w=[bank.res])
            resid_add(0, bank, half * 512, 512, ntok=NS)
        layer_norm(0, g, NS)
        make_xT(0, NS)
        ws.done()
        ws.done()

    s0_prefetched = []

    def prefetch_s0():
        for b in range(2):
            kb.dma(SP, V(xtok[1 + b].ap, 0, [[256, 4], [1, 256]]), sg[b].rearrange("h d v -> d h v"),
                   w=[xtok[1 + b].res])
        s0_prefetched.append(1)

    def sample_tile():
        kb.dma(SP, xtok[0].ap[0:NS, :], xs, w=[xtok[0].res])
        make_xT(0, NS)
        attn_sample()
        if nlayers > 1 and SLV >= 7:
            prefetch_s0()
        if SLV >= 6:
            ffn_sample(0, 1)
        if nlayers > 1 and SLV >= 7:
            make_xT(0, NS)
            gla_sample()
            if SLV >= 8:
                ffn_sample(1, 3)
        kb.dma(SP, ys, xtok[0].ap[0:NS, :], r=[xtok[0].res])

    def prefetch_x(s, t):
        for j in range(4):
            n = 4 * t + j
            kb.dma(POOL, big4b[j].ap, xp[s, n * 128:(n + 1) * 128, :], w=[big4b[j].res])

    tiles = [(s, t) for s in range(nseq) for t in range(ntiles_per_seq)]
    for ti, (s, t) in enumerate(tiles):
        if True:
            for j in range(4):
                n = 4 * t + j
                kb.dma(SP, xtok[j].ap, xp[s, n * 128:(n + 1) * 128, :], w=[xtok[j].res])
            for j in range(4):
                make_xT(j, src_bf=big4b[j])
            attn_prompt(s, t)
            if nlayers == 1 and ti + 1 < len(tiles):
                prefetch_x(*tiles[ti + 1])
            def store_y(j, s=s, t=t):
                n = 4 * t + j
                kb.dma(SP, yp[s, n * 128:(n + 1) * 128, :], xtok[j].ap, r=[xtok[j].res])

            if nlayers > 1:
                ffn_prompt(0, s, t, 1, after=make_xT)
                gla_prompt(s, t)
                nxt = (lambda ti=ti: prefetch_x(*tiles[ti + 1])) if ti + 1 < len(tiles) else None
                ffn_prompt(1, s, t, 3, after=store_y, mid=nxt)
            else:
                ffn_prompt(0, s, t, 1, after=store_y)

    if do_sample:
        sample_tile()

    for sl in kb.slots:
        if sl.cnt > 0:
            SP.eng.wait_ge(sl.sem, sl.cnt)
    for q in (PE, ACT, DVE, POOL):
        if q.cnt > 0:
            SP.eng.wait_ge(q.sem, q.cnt)
    return kb


def _consts():
    c = {}
    c["c_ident"] = np.eye(128, dtype=np.float32)
    inv = (1.0 / (10000.0 ** (np.arange(0, 64, 2, dtype=np.float32) / np.float32(64)))).astype(np.float32)

    def tables(pos):
        ang = pos.astype(np.float32)[:, None] * inv[None, :]
        cs = np.cos(ang).astype(np.float32)
        sn = np.sin(ang).astype(np.float32)
        return np.concatenate([cs, cs], -1), np.concatenate([-sn, sn], -1)
    pc, ps_ = tables(np.arange(SEQ, dtype=np.float32))
    c["c_ropec"] = np.ascontiguousarray(pc.reshape(16, 128, 64).transpose(1, 0, 2))
    c["c_ropes"] = np.ascontiguousarray(ps_.reshape(16, 128, 64).transpose(1, 0, 2))
    sc_, ss_ = tables(np.full((NSMP,), 16384.0, dtype=np.float32))
    c["c_ropecs"] = sc_
    c["c_ropess"] = ss_
    sidx = np.arange(128)[:, None]
    qidx = np.arange(128)[None, :]
    mp = np.where(sidx > qidx, 1.0, 0.0).astype(np.float32)
    mc = np.where(sidx <= qidx, 1.0, 0.0).astype(np.float32)
    c["c_maskp"] = np.tile(mp, (1, 4))
    c["c_maskc"] = np.tile(mc, (1, 4))
    c["c_trim"] = np.tile((sidx <= qidx).astype(np.float32), (1, 4))
    c["c_utri"] = np.where(sidx <= qidx, -1.0 / 16.0, 0.0).astype(np.float32)
    c["c_lstr"] = np.where(sidx > qidx, -1.0 / 16.0, 0.0).astype(np.float32)
    c["c_eye16"] = np.eye(NSMP, dtype=np.float32)
    return c


_CACHE = {}


def kernel(x_prompt, x_sample, cache_k, cache_v, state_gla, state_conv,
           attn_w_qkv, attn_b_qkv, attn_sinks, attn_w_o,
           gla_w_in, gla_w_a1, gla_w_a2, gla_b_a, gla_norm_g, gla_w_o,
           ffn_w_up, ffn_conv_w, ffn_conv_b, ffn_w_down,
           ln_mix_g, ln_mix_b, ln_ffn_g, ln_ffn_b):
    f = lambda a: np.ascontiguousarray(np.asarray(a, dtype=np.float32))
    nt = int(os.environ.get("MK_NT", "4"))
    nsq = int(os.environ.get("MK_NSEQ", "2"))
    nl = int(os.environ.get("MK_NL", "2"))
    smp = int(os.environ.get("MK_SMP", "1")) == 1
    key = (nt, nsq, nl, smp)
    if key not in _CACHE:
        _CACHE[key] = build(nt, nsq, smp, nl)
    kb = _CACHE[key]
    cwt = np.concatenate([f(ffn_conv_w), f(ffn_conv_b)[:, None, :]], axis=1)
    cwt = np.ascontiguousarray(cwt.reshape(2, 4, NCH, 128).transpose(0, 3, 2, 1))
    lng = np.ascontiguousarray(np.stack([f(ln_mix_g)[0], f(ln_ffn_g)[0], f(ln_mix_g)[1], f(ln_ffn_g)[1]]))
    lnb = np.ascontiguousarray(np.stack([f(ln_mix_b)[0], f(ln_ffn_b)[0], f(ln_mix_b)[1], f(ln_ffn_b)[1]]))
    shared = dict(
        wqkv=f(attn_w_qkv)[0], bqkv=f(attn_b_qkv), sinks=f(attn_sinks), woa=f(attn_w_o)[0],
        win=f(gla_w_in)[0], wa1=f(gla_w_a1)[0], wa2=f(gla_w_a2)[0], ba=f(gla_b_a), ng=f(gla_norm_g),
        wog=f(gla_w_o)[0], wup=f(ffn_w_up), cwt=cwt, wdn=f(ffn_w_down), lng=lng, lnb=lnb)
    shared.update(_consts())
    xp = f(x_prompt)
    xs = f(x_sample)
    ck = f(cache_k)
    cv = f(cache_v)
    sg = f(state_gla)
    sc = f(state_conv)
    in_maps = []
    for c in range(NCORES):
        m = dict(shared)
        m["xp"] = xp[NSEQ * c:NSEQ * (c + 1)]
        m["xs"] = xs[NSMP * c:NSMP * (c + 1), 0, :]
        m["ck"] = ck[0, NSMP * c:NSMP * (c + 1)].reshape(NSMP, 128, 128)
        m["cv"] = cv[0, NSMP * c:NSMP * (c + 1)].reshape(NSMP, 128, 128)
        m["sg"] = sg[0, NSMP * c:NSMP * (c + 1)]
        m["sc"] = np.ascontiguousarray(sc[:, NSMP * c:NSMP * (c + 1)])
        in_maps.append({k: np.ascontiguousarray(v) for k, v in m.items()})
    res = run_bass_kernel_spmd(kb.nc, in_maps, core_ids=list(range(NCORES)))
    R = res.results
    cat = lambda k, ax=0: np.concatenate([r[k] for r in R], axis=ax)
    y_prompt = cat("yp")
    y_sample = cat("ys").reshape(128, 1, D)
    nkp = cat("nkp").reshape(1, 16, 128, 2, 64)
    nvp = cat("nvp").reshape(1, 16, 128, 2, 64)
    nks = cat("nks").reshape(1, 128, 128, 2, 64)
    nvs = cat("nvs").reshape(1, 128, 128, 2, 64)
    gpo = cat("gp").reshape(1, 16, 4, 128, 256)
    gso = cat("gs").reshape(1, 128, 4, 128, 256)
    cpo = cat("cpo", 1)
    cso = cat("cso", 1)
    return (y_prompt, y_sample, nkp, nvp, nks, nvs, gpo, gso, cpo, cso)
```
